# Optimizing a Trainium2 kernel written in Bass

```python
import math
import jax, jax.numpy as jnp
from jax import lax
import numpy as np

D_MODEL = 2048
BATCH = 8
SEQ = 4096
DEPTH = 2

CTX_LEN = 256
GRID_W = 64
N_MIXERS = 2
N_FOURIER_LAYERS = (DEPTH + 1) // 2
N_ATTN_LAYERS = DEPTH // 2
FOURIER_GROUPS = 8
FOURIER_GROUP_DIM = D_MODEL // FOURIER_GROUPS
HEAD_DIM = 128
N_HEADS = D_MODEL // HEAD_DIM
N_KV_HEADS = 4
GQA_GROUP = N_HEADS // N_KV_HEADS
Q_DIM = N_HEADS * HEAD_DIM
KV_DIM = N_KV_HEADS * HEAD_DIM
ROPE_FREQS = HEAD_DIM // 4
ROPE_THETA = 10000.0
Q_BLOCK = 128
D_FF = 5632
N_MOD = 9
EPS = 1e-6

kernel_name = "hybrid_fourier_gqa_macaron_dit"


def rmsnorm(x, g):
    xf = x.astype(jnp.float32)
    y = xf * lax.rsqrt(jnp.mean(xf * xf, axis=-1, keepdims=True) + EPS)
    return (y * g.astype(jnp.float32)).astype(x.dtype)


def modulate(x, g, shift, scale):
    return rmsnorm(x, g) * (1.0 + scale) + shift


def swiglu(h, w_in, w_out):
    gate, up = jnp.split(h @ w_in, 2, axis=-1)
    return (jax.nn.silu(gate) * up) @ w_out


def axial_rope_tables(n_tokens):
    rows = n_tokens // GRID_W
    row = jnp.repeat(jnp.arange(rows), GRID_W).astype(jnp.float32)
    col = jnp.tile(jnp.arange(GRID_W), rows).astype(jnp.float32)
    inv_freq = ROPE_THETA ** (-jnp.arange(ROPE_FREQS, dtype=jnp.float32) / ROPE_FREQS)
    a_r = row[:, None] * inv_freq
    a_c = col[:, None] * inv_freq
    ang = jnp.concatenate([a_r, a_r, a_c, a_c], axis=-1)
    return jnp.cos(ang), jnp.sin(ang)


def apply_axial_rope(x, cos, sin):
    xs = x.astype(jnp.float32).reshape(*x.shape[:-1], 2, 2, ROPE_FREQS)
    x1, x2 = xs[..., 0, :], xs[..., 1, :]
    rot = jnp.stack([-x2, x1], axis=-2).reshape(x.shape)
    return (x.astype(jnp.float32) * cos[:, None, :] + rot * sin[:, None, :]).astype(x.dtype)


def fourier_mix(h, w_out):
    b, n, _ = h.shape
    hg = h.astype(jnp.float32).reshape(b, n, FOURIER_GROUPS, FOURIER_GROUP_DIM)
    f = jnp.fft.fft2(hg, axes=(1, 3), norm="ortho").real
    return f.reshape(b, n, D_MODEL).astype(h.dtype) @ w_out


def gqa_mix(h_lat, h_ctx, w_qkv, q_g, k_g, w_o, cos, sin, need_ctx):
    b, s, _ = h_lat.shape
    l = h_ctx.shape[1]
    scale = 1.0 / math.sqrt(HEAD_DIM)

    q_l, k_l, v_l = jnp.split(h_lat @ w_qkv, [Q_DIM, Q_DIM + KV_DIM], axis=-1)
    q_l = apply_axial_rope(rmsnorm(q_l.reshape(b, s, N_HEADS, HEAD_DIM), q_g), cos, sin)
    k_l = apply_axial_rope(rmsnorm(k_l.reshape(b, s, N_KV_HEADS, HEAD_DIM), k_g), cos, sin)
    v_l = v_l.reshape(b, s, N_KV_HEADS, HEAD_DIM)

    if need_ctx:
        q_c, k_c, v_c = jnp.split(h_ctx @ w_qkv, [Q_DIM, Q_DIM + KV_DIM], axis=-1)
    else:
        k_c, v_c = jnp.split(h_ctx @ w_qkv[:, Q_DIM:], [KV_DIM], axis=-1)
    k_c = rmsnorm(k_c.reshape(b, l, N_KV_HEADS, HEAD_DIM), k_g)
    v_c = v_c.reshape(b, l, N_KV_HEADS, HEAD_DIM)

    k_all = jnp.concatenate([k_c, k_l], axis=1).transpose(0, 2, 1, 3)
    v_all = jnp.concatenate([v_c, v_l], axis=1).transpose(0, 2, 1, 3)

    n_blk = s // Q_BLOCK
    qb = q_l.reshape(b, n_blk, Q_BLOCK, N_KV_HEADS, GQA_GROUP, HEAD_DIM).transpose(1, 0, 3, 4, 2, 5)

    def attend(q_blk):
        sc = jnp.einsum('bkgqd,bkld->bkgql', q_blk, k_all).astype(jnp.float32) * scale
        p = jax.nn.softmax(sc, axis=-1).astype(v_all.dtype)
        return jnp.einsum('bkgql,bkld->bkgqd', p, v_all)

    o = lax.map(attend, qb)
    o = o.transpose(1, 0, 4, 2, 3, 5).reshape(b, s, Q_DIM)
    out_lat = o @ w_o

    out_ctx = None
    if need_ctx:
        qc = rmsnorm(q_c.reshape(b, l, N_HEADS, HEAD_DIM), q_g)
        qc = qc.reshape(b, l, N_KV_HEADS, GQA_GROUP, HEAD_DIM).transpose(0, 2, 3, 1, 4)
        kc = k_c.transpose(0, 2, 1, 3)
        vc = v_c.transpose(0, 2, 1, 3)
        sc = jnp.einsum('bkgqd,bkld->bkgql', qc, kc).astype(jnp.float32) * scale
        p = jax.nn.softmax(sc, axis=-1).astype(vc.dtype)
        oc = jnp.einsum('bkgql,bkld->bkgqd', p, vc).transpose(0, 3, 1, 2, 4).reshape(b, l, Q_DIM)
        out_ctx = oc @ w_o
    return out_lat, out_ctx


def setup_inputs(seed: int = 0) -> dict:
    key = jax.random.key(seed)
    ks = jax.random.split(key, 14)
    f32 = jnp.float32
    d = D_MODEL
    nrm = lambda k, shape, s: jax.random.normal(k, shape, f32) * s
    return {
        "x": nrm(ks[0], (BATCH, SEQ, d), 1.0),
        "c": nrm(ks[1], (BATCH, d), 1.0),
        "ctx": nrm(ks[2], (BATCH, CTX_LEN, d), 1.0),
        "c_ctx": nrm(ks[3], (d,), 1.0),
        "w_ada": nrm(ks[4], (DEPTH, d, N_MOD * d), 0.5 * d ** -0.5),
        "b_ada": nrm(ks[5], (DEPTH, N_MOD * d), 0.01),
        "norm_g": 1.0 + nrm(ks[6], (DEPTH, 3, d), 0.01),
        "w_ffn_in": nrm(ks[7], (DEPTH, 2, d, 2 * D_FF), d ** -0.5),
        "w_ffn_out": nrm(ks[8], (DEPTH, 2, D_FF, d), D_FF ** -0.5),
        "w_fourier_out": nrm(ks[9], (N_FOURIER_LAYERS, d, d), d ** -0.5),
        "w_qkv": nrm(ks[10], (N_ATTN_LAYERS, d, Q_DIM + 2 * KV_DIM), d ** -0.5),
        "q_norm_g": 1.0 + nrm(ks[11], (N_ATTN_LAYERS, HEAD_DIM), 0.01),
        "k_norm_g": 1.0 + nrm(ks[12], (N_ATTN_LAYERS, HEAD_DIM), 0.01),
        "w_attn_out": nrm(ks[13], (N_ATTN_LAYERS, Q_DIM, d), Q_DIM ** -0.5),
    }


def reference(x, c, ctx, c_ctx, w_ada, b_ada, norm_g, w_ffn_in, w_ffn_out,
              w_fourier_out, w_qkv, q_norm_g, k_norm_g, w_attn_out):
    b, n, d = x.shape
    cos, sin = axial_rope_tables(n)
    s_c = jax.nn.silu(c)
    s_cc = jax.nn.silu(c_ctx)

    for i in range(DEPTH):
        last = i == DEPTH - 1
        mixer = i % N_MIXERS
        j = i // N_MIXERS
        m_l = (s_c @ w_ada[i] + b_ada[i]).reshape(b, N_MOD, d)[:, :, None, :]
        m_c = (s_cc @ w_ada[i] + b_ada[i]).reshape(N_MOD, d)
        mod_l = [m_l[:, k] for k in range(N_MOD)]
        mod_c = [m_c[k] for k in range(N_MOD)]
        ctx_feeds_latent = (not last) or mixer == 1

        x = x + 0.5 * mod_l[2] * swiglu(modulate(x, norm_g[i, 0], mod_l[0], mod_l[1]),
                                        w_ffn_in[i, 0], w_ffn_out[i, 0])
        if ctx_feeds_latent:
            ctx = ctx + 0.5 * mod_c[2] * swiglu(modulate(ctx, norm_g[i, 0], mod_c[0], mod_c[1]),
                                                w_ffn_in[i, 0], w_ffn_out[i, 0])

        h_l = modulate(x, norm_g[i, 1], mod_l[3], mod_l[4])
        if mixer == 0:
            x = x + mod_l[5] * fourier_mix(h_l, w_fourier_out[j])
            if not last:
                h_c = modulate(ctx, norm_g[i, 1], mod_c[3], mod_c[4])
                ctx = ctx + mod_c[5] * fourier_mix(h_c, w_fourier_out[j])
        else:
            h_c = modulate(ctx, norm_g[i, 1], mod_c[3], mod_c[4])
            o_l, o_c = gqa_mix(h_l, h_c, w_qkv[j], q_norm_g[j], k_norm_g[j], w_attn_out[j],
                               cos, sin, need_ctx=not last)
            x = x + mod_l[5] * o_l
            if not last:
                ctx = ctx + mod_c[5] * o_c

        x = x + 0.5 * mod_l[8] * swiglu(modulate(x, norm_g[i, 2], mod_l[6], mod_l[7]),
                                        w_ffn_in[i, 1], w_ffn_out[i, 1])
        if not last:
            ctx = ctx + 0.5 * mod_c[8] * swiglu(modulate(ctx, norm_g[i, 2], mod_c[6], mod_c[7]),
                                                w_ffn_in[i, 1], w_ffn_out[i, 1])
    return x
```

```python
import math
from contextlib import ExitStack

import numpy as np
import ml_dtypes

import concourse.bass as bass
import concourse.mybir as mybir
from concourse.bass_utils import run_bass_kernel_spmd

F32 = mybir.dt.float32
BF16 = mybir.dt.bfloat16
AF = mybir.ActivationFunctionType
ALU = mybir.AluOpType
EPS = 1e-6
P = 128


class Cfg:
    def __init__(self, D=2048, FF=5632, S=4096, L=256, NH=16, NKV=4, FG=8, T=512,
                 GRID_W=64, THETA=10000.0):
        self.D, self.FF, self.S, self.L, self.NH, self.NKV, self.FG, self.T = D, FF, S, L, NH, NKV, FG, T
        self.GRID_W, self.THETA = GRID_W, THETA
        self.HD = 128
        self.DC = D // P
        self.FC = FF // P
        self.G = NH // NKV
        assert self.G == 4 and NH * 128 == D and D % 512 == 0 and self.FC % 2 == 0
        self.FGD = D // FG
        assert self.FGD == 256
        self.NT = S + L
        self.NG = S // T
        assert S % T == 0 and T == 512 and L % 128 == 0 and L <= 512
        self.NMOD = 9
        self.n_ada = 9 * D // 512
        self.n_fin = self.FC // 2
        self.n_sq = D // 512
        self.n_qkv = D // 512 + 2
        off = 0
        self.a_ada = [off, off + self.n_ada]; off += 2 * self.n_ada
        self.a_fin = {}
        for l in range(2):
            for i in range(2):
                self.a_fin[(l, i)] = off; off += self.n_fin
        self.a_wf = off; off += self.n_sq
        self.a_qkv = off; off += self.n_qkv
        self.a_wo = off; off += self.n_sq
        self.n_a = off
        self.b_fout = {}
        off = 0
        for l in range(2):
            for i in range(2):
                self.b_fout[(l, i)] = off; off += self.DC
        self.n_b = off


class Tok:
    __slots__ = ("name", "w", "rs")

    def __init__(self, name):
        self.name = name
        self.w = None
        self.rs = {}


class Op:
    __slots__ = ("eng", "fn", "waits", "sigval", "dma_key", "dma_val", "is_dma")


ENGS = ("pe", "act", "dve", "pool", "sp")


class Rec:
    def __init__(self, nc, es):
        self.nc = nc
        self.es = es
        self.sems = {}
        self.cnt = {}
        self.ops = {e: [] for e in ENGS}
        self.waited = {e: {} for e in ENGS}
        for e in ENGS:
            self._sem("E_" + e)

    def _sem(self, key):
        if key not in self.sems:
            self.sems[key] = self.es.enter_context(self.nc.semaphore(key))
            self.cnt[key] = 0
        return self.sems[key]

    def _add_wait(self, op, key, val):
        w = self.waited[op.eng]
        if w.get(key, 0) >= val:
            return
        w[key] = val
        op.waits.append((key, val))

    def _dep(self, op, d, same_ok=False):
        if d is None:
            return
        if d.is_dma:
            self._add_wait(op, d.dma_key, self.cnt[d.dma_key])
        else:
            if d.eng == op.eng and not op.is_dma and (same_ok or op.eng == "pe"):
                return
            self._add_wait(op, "E_" + d.eng, d.sigval)

    def _mk(self, eng, fn, reads, writes, is_dma):
        op = Op()
        op.eng, op.fn, op.waits, op.is_dma = eng, fn, [], is_dma
        op.sigval = op.dma_key = op.dma_val = None
        for t in reads:
            self._dep(op, t.w)
        for t in writes:
            self._dep(op, t.w)
            for r in t.rs.values():
                self._dep(op, r, same_ok=True)
        for t in writes:
            t.w = op
            t.rs = {}
        for t in reads:
            t.rs[eng if not is_dma else ("dma", id(op))] = op
        self.ops[eng].append(op)
        return op

    def op(self, eng, fn, reads=(), writes=()):
        o = self._mk(eng, fn, reads, writes, False)
        key = "E_" + eng
        self.cnt[key] += 1
        o.sigval = self.cnt[key]
        return o

    def dma(self, queue, key, fn, n, reads=(), writes=()):
        key = "D_" + key
        self._sem(key)
        o = self._mk(queue, fn, reads, writes, True)
        o.dma_key = key
        self.cnt[key] += 16 * n
        o.dma_val = n
        return o

    def barrier(self):
        for e in ENGS:
            o = Op()
            o.eng, o.fn, o.waits, o.is_dma = e, None, [], False
            o.sigval = o.dma_key = o.dma_val = None
            for key, c in self.cnt.items():
                if c > 0 and key != "E_" + e:
                    self._add_wait(o, key, c)
            self.ops[e].append(o)

    def flush(self):
        nc = self.nc
        ops = self.ops
        sems = self.sems
        self.ops = {e: [] for e in ENGS}

        def run(eng, lst, ekey):
            sem_e = sems[ekey]
            for o in lst:
                for key, val in o.waits:
                    eng.wait_ge(sems[key], val)
                if o.fn is None:
                    continue
                if o.is_dma:
                    ins = o.fn(eng)
                    assert len(ins) == o.dma_val, (len(ins), o.dma_val)
                    s = sems[o.dma_key]
                    for i in ins:
                        i.then_inc(s, 16)
                else:
                    i = o.fn(eng)
                    i.then_inc(sem_e, 1)

        with nc.Block() as block:
            @block.tensor
            def _(e):
                run(e, ops["pe"], "E_pe")

            @block.scalar
            def _(e):
                run(e, ops["act"], "E_act")

            @block.vector
            def _(e):
                run(e, ops["dve"], "E_dve")

            @block.gpsimd
            def _(e):
                run(e, ops["pool"], "E_pool")

            @block.sync
            def _(e):
                run(e, ops["sp"], "E_sp")


class Ring:
    def __init__(self, rec, name, bufs, loader, queue, cache=None):
        self.rec, self.name, self.bufs, self.loader, self.queue = rec, name, bufs, loader, queue
        self.cache = cache
        self.n = len(bufs)
        self.toks = [Tok(f"{name}{i}") for i in range(self.n)]
        self.seq = []
        self.head = 0
        self.loaded = 0

    def start(self, seq):
        assert self.head == len(self.seq), (self.name, self.head, len(self.seq))
        self.seq = list(seq)
        self.head = 0
        self.loaded = 0
        for _ in range(min(self.n, len(self.seq))):
            self._load()

    def _load(self):
        i = self.loaded
        slot = i % self.n
        src = self.seq[i]
        buf = self.bufs[slot]
        ch = self.cache
        if ch is not None and isinstance(src, int) and src >= ch["base"]:
            if src in ch["seen"]:
                fn, n = ch["load_bf"](buf, src)
                self.rec.dma(self.queue, f"{self.name}{slot}", fn, n, reads=[ch["tok"]], writes=[self.toks[slot]])
            else:
                fn, n = self.loader(buf, src)
                self.rec.dma(self.queue, f"{self.name}{slot}", fn, n, writes=[self.toks[slot]])
                v = ch["visits"].get(src, 0)
                ch["visits"][src] = v + 1
                if v == src % 3:
                    fn2, n2 = ch["store"](buf, src)
                    self.rec.dma("sp", f"{self.name}wb", fn2, n2, reads=[self.toks[slot]], writes=[ch["tok"]])
                    ch["seen"].add(src)
        else:
            fn, n = self.loader(buf, src)
            self.rec.dma(self.queue, f"{self.name}{slot}", fn, n, writes=[self.toks[slot]])
        self.loaded += 1

    def get(self):
        slot = self.head % self.n
        return self.bufs[slot], self.toks[slot]

    def release(self):
        self.head += 1
        if self.loaded < len(self.seq):
            self._load()


def _split(n, cap=2048):
    a = 1
    while n % a or n // a > cap:
        a += 1
    return a


def build(cfg, upto=99):
    c = cfg
    D, FF, S, L, NT, T, DC, FC = c.D, c.FF, c.S, c.L, c.NT, c.T, c.DC, c.FC
    NH, NKV = c.NH, c.NKV
    KVW = NKV * P
    nc = bass.Bass("TRN2", target_bir_lowering=False)

    def din(name, shape, dt):
        return nc.dram_tensor(name, list(shape), dt, kind="ExternalInput").ap()

    xT0 = din("xT0", [D, NT], F32)
    WA = din("WA", [c.n_a * P, DC * 512], F32)
    WB = din("WB", [c.n_b * P, FC * 128], F32)
    cc_d = din("cc", [P, DC * 2], F32)
    bada_d = din("bada", [P, 2 * 9 * DC * 2], F32)
    ng_d = din("ng", [P, 2 * 3 * DC], F32)
    qkg_d = din("qkg", [P, 2], F32)
    csc_d = din("csc", [P, 2 * 512], BF16)
    rope_d = din("rope", [P, 2 * S], F32)
    rm_d = din("rm", [P, P], F32)
    NCH_S = S // P
    KH_S = (S // 2) // P + 1
    NHALF_S = 1
    NB_S = S // 512
    dftS_d = din("dftS", [NB_S * 2 * NHALF_S * P, KH_S * 512], BF16)
    NCH_L = L // P
    dftL_d = din("dftL", [2 * P, NCH_L * L], BF16)
    outT = nc.dram_tensor("outT", [D, S], F32, kind="ExternalOutput").ap()
    xs = nc.dram_tensor("xs", [D, NT], F32).ap()
    h1 = nc.dram_tensor("h1", [D, NT], BF16).ap()
    zt = nc.dram_tensor("zt", [D, NT], BF16).ap()
    qt = nc.dram_tensor("qt", [P, NKV * (S // P) * 4 * P], BF16).ap()
    kt = nc.dram_tensor("kt", [P, NKV * NT], BF16).ap()
    vv = nc.dram_tensor("vv", [NT, KVW], BF16).ap()
    a0 = c.a_fin[(0, 0)]
    WAc = nc.dram_tensor("WAc", [(c.n_a - a0) * P, DC * 512], BF16).ap()
    WBc = nc.dram_tensor("WBc", [c.n_b * P, FC * 128], BF16).ap()

    def fm(ap):
        return ap.rearrange("(k p) n -> p k n", p=P)

    es = ExitStack()
    with es:
        R = Rec(nc, es)

        uid = [0]

        def sb(st, name, shape, dt):
            uid[0] += 1
            return st.enter_context(nc.sbuf_tensor(f"s{uid[0]}_{name}", list(shape), dt))

        ones = sb(es, "ones", [P, P], BF16)
        sT = sb(es, "sT", [P, DC, 2], BF16)
        ccs = sb(es, "ccs", [P, DC, 2], F32)
        mod = sb(es, "mod", [P, 2, 9, DC, 2], F32)
        bada = sb(es, "bada", [P, 2, 9, DC, 2], F32)
        ng = sb(es, "ng", [P, 2, 3, DC], F32)
        coefA = sb(es, "coefA", [P, 2, 3, 2, DC], F32)
        coefB = sb(es, "coefB", [P, 2, 3, 2, DC], F32)
        coefG = sb(es, "coefG", [P, 2, 3, 2, DC], F32)
        qkg = sb(es, "qkg", [P, 2], F32)
        rm = sb(es, "rm", [P, P], F32)
        epsc = sb(es, "epsc", [P, 1], F32)
        pss = [es.enter_context(nc.psum_tensor(f"ps{i}", [P, 512], F32)) for i in range(8)]

        t_ones, t_sT, t_cc, t_mod, t_bada, t_ng, t_coef, t_qkg, t_rm = (Tok(n) for n in
            ("ones", "sT", "cc", "mod", "bada", "ng", "coef", "qkg", "rm"))
        t_x = [Tok(f"x{k}") for k in range(DC)]
        t_h = [Tok(f"h{k}") for k in range(DC)]
        t_sq = [Tok("sq0"), Tok("sq1")]
        t_nt = [Tok("nt0"), Tok("nt1")]
        t_rstd = [Tok("rstd0"), Tok("rstd1")]
        t_rscr = [Tok("rscr0"), Tok("rscr1")]
        t_ps = [Tok(f"ps{i}") for i in range(8)]

        xT = hT = wA_t = sq = nt_ = rstd = ringA = rscr = None
        nsubA = _split(DC * 512)

        def loadA(buf, idx):
            def fn(e):
                src = WA[idx * P:(idx + 1) * P, :].rearrange("p (a b) -> p a b", a=nsubA)
                dst = buf.rearrange("p k n -> p (k n)").rearrange("p (a b) -> p a b", a=nsubA)
                return [e.dma_start(out=dst, in_=src)]
            return fn, 1

        cacheA = dict(base=a0, seen=set(), visits={}, tok=Tok("cacheA"),
                      load_bf=lambda buf, idx: (lambda e: [e.dma_start(
                          out=buf.rearrange("p k n -> p (k n)"), in_=WAc[(idx - a0) * P:(idx - a0 + 1) * P, :])], 1),
                      store=lambda buf, idx: (lambda e: [e.dma_start(
                          out=WAc[(idx - a0) * P:(idx - a0 + 1) * P, :], in_=buf.rearrange("p k n -> p (k n)"))], 1))
        cacheB = dict(base=0, seen=set(), visits={}, tok=Tok("cacheB"),
                      load_bf=lambda buf, idx: (lambda e: [e.dma_start(
                          out=buf.rearrange("p k n -> p (k n)"), in_=WBc[idx * P:(idx + 1) * P, :])], 1),
                      store=lambda buf, idx: (lambda e: [e.dma_start(
                          out=WBc[idx * P:(idx + 1) * P, :], in_=buf.rearrange("p k n -> p (k n)"))], 1))

        def alloc_common(st, with_x=True, nA=3):
            nonlocal xT, hT, wA_t, sq, nt_, rstd, ringA, rscr
            if with_x:
                xT = sb(st, "xT", [P, DC, T], F32)
                hT = sb(st, "hT", [P, DC, T], BF16)
            wA_t = sb(st, "wA", [P, nA, DC, 512], BF16)
            sq = sb(st, "sq", [P, 2, T], BF16)
            nt_ = sb(st, "nt", [P, 2, T], F32)
            rstd = sb(st, "rstd", [P, 2, T], F32)
            rscr = sb(st, "rscr", [P, 2, T], F32)
            ringA = Ring(R, "wA", [wA_t[:, i] for i in range(nA)], loadA, "pool", cache=cacheA)

        def vec(tile, l, i, r, k):
            return tile[:, l, i, r, k:k + 1]

        def mm_group(out_ap, pairs, reads, tok_out):
            n = len(pairs)

            def fn(e):
                ins = None
                for j, (a, b) in enumerate(pairs):
                    ins = e.matmul(out_ap, lhsT=a, rhs=b, start=(j == 0), stop=(j == n - 1))
                return ins
            return R.op("pe", fn, reads=reads, writes=[tok_out])

        def rms_rstd(src_chunks, src_toks, Tg, nfeat, ps_i, rb=0, out_ps=None):
            n = len(src_chunks)
            psn = pss[ps_i][:, :Tg]
            sq_, rstd_, rscr_ = sq, rstd, rscr
            for k in range(n):
                b = (k + rb) % 2
                R.op("act", lambda e, k=k, b=b: e.activation(out=sq_[:, b, :Tg], in_=src_chunks[k], func=AF.Square),
                     reads=[src_toks[k]], writes=[t_sq[b]])
                R.op("pe", lambda e, k=k, b=b: e.matmul(psn, lhsT=ones[:], rhs=sq_[:, b, :Tg],
                                                       start=(k == 0), stop=(k == n - 1)),
                     reads=[t_sq[b], t_ones], writes=[t_ps[ps_i]])
            R.op("act", lambda e: e.activation(out=rscr_[:, rb, :Tg], in_=psn, func=AF.Ln,
                                               bias=epsc[:, 0:1], scale=1.0 / nfeat),
                 reads=[t_ps[ps_i]], writes=[t_rscr[rb]])
            if out_ps is None:
                R.op("act", lambda e: e.activation(out=rstd_[:, rb, :Tg], in_=rscr_[:, rb, :Tg], func=AF.Exp, scale=-0.5),
                     reads=[t_rscr[rb]], writes=[t_rstd[rb]])
            else:
                R.op("act", lambda e: e.activation(out=pss[out_ps][:, :Tg], in_=rscr_[:, rb, :Tg], func=AF.Exp, scale=-0.5),
                     reads=[t_rscr[rb]], writes=[t_ps[out_ps]])

        def norm_mod(l, i, r, Tg):
            xT_, hT_, nt2, rstd_ = xT, hT, nt_, rstd
            rms_rstd([xT_[:, k, :Tg] for k in range(DC)], t_x, Tg, D, 0, out_ps=7)
            for k in range(DC):
                b = k % 2
                R.op("dve", lambda e, k=k, b=b: e.tensor_tensor(out=nt2[:, b, :Tg], in0=xT_[:, k, :Tg],
                                                                in1=pss[7][:, :Tg], op=ALU.mult),
                     reads=[t_x[k], t_ps[7]], writes=[t_nt[b]])
                R.op("act", lambda e, k=k, b=b: e.activation(out=hT_[:, k, :Tg], in_=nt2[:, b, :Tg], func=AF.Identity,
                                                             scale=vec(coefA, l, i, r, k), bias=vec(coefB, l, i, r, k)),
                     reads=[t_nt[b], t_coef], writes=[t_h[k]])

        def proj(nslab, inT, in_toks, Tg, consume, ps_ids=(1, 2), jmax=4):
            for s in range(nslab):
                buf, tk = ringA.get()
                for j in range(jmax):
                    oc = s * 4 + j
                    pi = ps_ids[oc % len(ps_ids)]
                    mm_group(pss[pi][:, :Tg],
                             [(buf[:, k, j * 128:(j + 1) * 128], inT[:, k, :Tg]) for k in range(DC)],
                             reads=[tk] + list(in_toks), tok_out=t_ps[pi])
                    consume(oc, pss[pi][:, :Tg], t_ps[pi])
                ringA.release()

        def resid_consume(l, i, r, Tg):
            xT_ = xT

            def consume(oc, ps, ptk):
                R.op("dve", lambda e: e.scalar_tensor_tensor(out=xT_[:, oc, :Tg], in0=ps, scalar=vec(coefG, l, i, r, oc),
                                                             in1=xT_[:, oc, :Tg], op0=ALU.mult, op1=ALU.add),
                     reads=[ptk, t_x[oc], t_coef], writes=[t_x[oc]])
            return consume

        NQ = 4 if DC % 4 == 0 else 1
        QC = DC // NQ

        def load_x(src, g0, Tg):
            xT_ = xT
            for q in range(NQ):
                ks = slice(q * QC, (q + 1) * QC)
                R.dma("sp", f"x{q}", lambda e, ks=ks: [e.dma_start(out=xT_[:, ks, :Tg], in_=fm(src)[:, ks, g0:g0 + Tg])], 1,
                      writes=t_x[ks])

        def store_x(dst, g0, Tg):
            xT_ = xT
            for q in range(NQ):
                ks = slice(q * QC, (q + 1) * QC)
                R.dma("sp", f"x{q}", lambda e, ks=ks: [e.dma_start(out=fm(dst)[:, ks, g0:g0 + Tg], in_=xT_[:, ks, :Tg])], 1,
                      reads=t_x[ks])

        def groups(with_ctx=True):
            gl = [(g * T, T, 0) for g in range(c.NG)]
            if with_ctx:
                gl.append((S, L, 1))
            return gl

        def debug_tail(src, bf=False):
            with ExitStack() as st:
                alloc_common(st)
                xT_, hT_ = xT, hT
                for g in range(S // T):
                    g0 = g * T
                    if bf:
                        R.dma("sp", "h", lambda e, g0=g0: [e.dma_start(out=hT_[:, :, :], in_=fm(src)[:, :, g0:g0 + T])], 1, writes=t_h)
                        for k in range(DC):
                            R.op("dve", lambda e, k=k: e.tensor_copy(out=xT_[:, k, :], in_=hT_[:, k, :]), reads=[t_h[k]], writes=[t_x[k]])
                    else:
                        load_x(src, g0, T)
                    store_x(outT, g0, T)
                R.barrier()
                R.flush()

        modf = mod.rearrange("p l j k r -> p (l j k) r")
        badaf = bada.rearrange("p l j k r -> p (l j k) r")

        def ada_slab(ring, l, s, pi):
            buf, tk = ring.get()
            for j in range(4):
                mm_group(pss[pi][:, 2 * j:2 * j + 2],
                         [(buf[:, k, j * 128:(j + 1) * 128], sT[:, k, :]) for k in range(DC)],
                         reads=[tk, t_sT], tok_out=t_ps[pi])
            c0 = (l * c.n_ada + s) * 4
            R.op("dve", lambda e: e.tensor_tensor(
                    out=modf[:, c0:c0 + 4, :], in0=pss[pi][:, 0:8].rearrange("p (j r) -> p j r", r=2),
                    in1=badaf[:, c0:c0 + 4, :], op=ALU.add),
                 reads=[t_ps[pi], t_bada], writes=[t_mod])
            ring.release()

        def ada_coefs(l):
            for i in range(3):
                for r in range(2):
                    R.op("dve", lambda e, i=i, r=r: e.scalar_tensor_tensor(
                            out=coefA[:, l, i, r, :], in0=mod[:, l, 3 * i + 1, :, r], scalar=1.0,
                            in1=ng[:, l, i, :], op0=ALU.add, op1=ALU.mult),
                         reads=[t_mod, t_ng], writes=[t_coef])
                    R.op("dve", lambda e, i=i, r=r: e.tensor_copy(out=coefB[:, l, i, r, :], in_=mod[:, l, 3 * i, :, r]),
                         reads=[t_mod], writes=[t_coef])
                    gs = 1.0 if i == 1 else 0.5
                    R.op("dve", lambda e, i=i, r=r, gs=gs: e.tensor_scalar(
                            out=coefG[:, l, i, r, :], in0=mod[:, l, 3 * i + 2, :, r], scalar1=gs, scalar2=None,
                            op0=ALU.mult),
                         reads=[t_mod], writes=[t_coef])

        with ExitStack() as st:
            alloc_common(st, with_x=False)
            R.op("pool", lambda e: e.memset(ones[:], 1.0), writes=[t_ones])
            R.op("pool", lambda e: e.memset(epsc[:], EPS), writes=[t_ones])
            R.dma("sp", "c0", lambda e: [e.dma_start(out=ccs.rearrange("p k r -> p (k r)"), in_=cc_d),
                                         e.dma_start(out=bada.rearrange("p a b k r -> p (a b k r)"), in_=bada_d),
                                         e.dma_start(out=ng.rearrange("p a b k -> p (a b k)"), in_=ng_d),
                                         e.dma_start(out=qkg[:], in_=qkg_d),
                                         e.dma_start(out=rm[:], in_=rm_d)], 5,
                  writes=[t_cc, t_bada, t_ng, t_qkg, t_rm])
            R.op("act", lambda e: e.activation(out=sT[:], in_=ccs[:], func=AF.Silu), reads=[t_cc], writes=[t_sT])
            ringA.start([c.a_ada[0] + s for s in range(c.n_ada)])
            for s in range(c.n_ada):
                ada_slab(ringA, 0, s, 1 + (s % 2))
            ada_coefs(0)
            R.barrier()
            R.flush()

        t_a = [Tok(f"a{f}") for f in range(FC)]
        t_sg = [Tok("sg0"), Tok("sg1")]
        nsubB = _split(FC * 128)

        def ffn_phase(st, nB=2):
            aT = sb(st, "aT", [P, FC, T], BF16)
            wB_t = sb(st, "wB", [P, nB, FC, 128], BF16)
            sg = sb(st, "sg", [P, 2, T], F32)
            xT_, hT_ = xT, hT

            def loadB(buf, idx):
                def fn(e):
                    src = WB[idx * P:(idx + 1) * P, :].rearrange("p (a b) -> p a b", a=nsubB)
                    dst = buf.rearrange("p k n -> p (k n)").rearrange("p (a b) -> p a b", a=nsubB)
                    return [e.dma_start(out=dst, in_=src)]
                return fn, 1

            ringB = Ring(R, "wB", [wB_t[:, i] for i in range(nB)], loadB, "pool", cache=cacheB)

            def ffn(l, i, r, Tg):
                sub_i = 0 if i == 0 else 2
                norm_mod(l, sub_i, r, Tg)
                for s in range(c.n_fin):
                    buf, tk = ringA.get()
                    for j in range(2):
                        f = 2 * s + j
                        pg, pu = 3 + (f % 2), 5 + (f % 2)
                        if f == 0:
                            for k in range(DC):
                                R.op("pe", lambda e, k=k, buf=buf, pg=pg: e.matmul(
                                        pss[pg][:, :Tg], lhsT=buf[:, k, 0:128], rhs=hT_[:, k, :Tg],
                                        start=(k == 0), stop=(k == DC - 1)),
                                     reads=[tk, t_h[k]], writes=[t_ps[pg]])
                        else:
                            mm_group(pss[pg][:, :Tg], [(buf[:, k, j * 128:(j + 1) * 128], hT_[:, k, :Tg]) for k in range(DC)],
                                     reads=[tk] + t_h, tok_out=t_ps[pg])
                        mm_group(pss[pu][:, :Tg], [(buf[:, k, 256 + j * 128:256 + (j + 1) * 128], hT_[:, k, :Tg]) for k in range(DC)],
                                 reads=[tk] + t_h, tok_out=t_ps[pu])
                        b = f % 2
                        R.op("act", lambda e, pg=pg, b=b: e.activation(out=sg[:, b, :Tg], in_=pss[pg][:, :Tg], func=AF.Silu),
                             reads=[t_ps[pg]], writes=[t_sg[b]])
                        R.op("dve", lambda e, pu=pu, b=b, f=f: e.tensor_tensor(out=aT[:, f, :Tg], in0=pss[pu][:, :Tg],
                                                                              in1=sg[:, b, :Tg], op=ALU.mult),
                             reads=[t_ps[pu], t_sg[b]], writes=[t_a[f]])
                    ringA.release()
                cons = resid_consume(l, sub_i, r, Tg)
                for oc in range(DC):
                    buf, tk = ringB.get()
                    pi = 1 + (oc % 2)
                    mm_group(pss[pi][:, :Tg], [(buf[:, f, :], aT[:, f, :Tg]) for f in range(FC)],
                             reads=[tk] + t_a, tok_out=t_ps[pi])
                    cons(oc, pss[pi][:, :Tg], t_ps[pi])
                    ringB.release()
            return ffn, ringB, aT, sg

        def fin_seq(l, i):
            return [c.a_fin[(l, i)] + s for s in range(c.n_fin)]

        def fout_seq(l, i):
            return [c.b_fout[(l, i)] + s for s in range(DC)]

        with ExitStack() as st:
            alloc_common(st)
            ffn, ringB, aT, sg = ffn_phase(st, nB=3)
            gl = groups()
            ringA.start([s for _ in gl for s in fin_seq(0, 0)])
            ringB.start([s for _ in gl for s in fout_seq(0, 0)])
            hT_ = hT
            for (g0, Tg, r) in gl:
                load_x(xT0, g0, Tg)
                ffn(0, 0, r, Tg)
                store_x(xs, g0, Tg)
                norm_mod(0, 1, r, Tg)
                R.dma("sp", "h", lambda e, g0=g0, Tg=Tg: [e.dma_start(out=fm(h1)[:, :, g0:g0 + Tg], in_=hT_[:, :, :Tg])], 1,
                      reads=t_h)
            R.barrier()
            R.flush()
        if upto <= 1:
            debug_tail(xs)
            return nc

        with ExitStack() as st:
            NHf = S // 2
            NHp = KH_S * P
            NCHm = max(KH_S, NCH_L)
            hB = sb(st, "hB", [P, 4, S], BF16)
            he = sb(st, "he", [P, 4, NHp], BF16)
            ho = sb(st, "ho", [P, 4, NHp], BF16)
            PQ = sb(st, "PQ", [P, NCHm, 2, 2, 256], BF16)
            wF_t = sb(st, "wF", [P, 3, NCHm, 512], BF16)
            wA2 = sb(st, "wA2", [P, 2, DC, 512], BF16)
            ringA2 = Ring(R, "wA", [wA2[:, i] for i in range(2)], loadA, "pool")
            ringA2.start([c.a_ada[1] + s for s in range(c.n_ada)])
            ada_left = list(range(c.n_ada))
            ztS = sb(st, "ztS", [P, 2, 4, 512], BF16)
            csc = sb(st, "csc", [P, 2, 512], BF16)
            t_hB, t_csc, t_he, t_ho = Tok("hB"), Tok("csc"), Tok("he"), Tok("ho")
            t_PQ = [Tok(f"PQ{n}") for n in range(NCHm)]
            t_zs = [Tok("zs0"), Tok("zs1")]
            R.dma("sp", "c0", lambda e: [e.dma_start(out=csc.rearrange("p a b -> p (a b)"), in_=csc_d)], 1, writes=[t_csc])
            R.op("pool", lambda e: e.memset(he[:, :, NHf + 1:NHp], 0.0), writes=[t_he])
            R.op("pool", lambda e: e.memset(ho[:, :, NHf:NHp], 0.0), writes=[t_ho])
            R.op("pool", lambda e: e.memset(ho[:, :, 0:1], 0.0), writes=[t_ho])

            def loadF(buf, src):
                tab, row, KH, BW = src

                def fn(e):
                    return [e.dma_start(out=buf[:, :KH, :BW], in_=tab[row * P:(row + 1) * P, :].rearrange("p (k n) -> p k n", k=KH))]
                return fn, 1

            ringF = Ring(R, "wF", [wF_t[:, i] for i in range(3)], loadF, "sp")
            NCB = D // 512
            zi = [0]

            def fourier(n, tok0, tab, KH, NHALF, NB, BW, fold, seq_only=False):
                NCHn = KH * NHALF
                if seq_only:
                    seq = []
                    for cb in range(NCB):
                        for nb in range(NB):
                            for cs in range(2):
                                for hf in range(NHALF):
                                    seq.append((tab, (nb * 2 + cs) * NHALF + hf, KH, BW))
                    return seq
                PQf = PQ.rearrange("p n a g c -> p n a (g c)")

                def prep(cb):
                    R.dma("sp", "hB", lambda e: [e.dma_start(out=hB[:, :, :n], in_=fm(h1)[:, cb * 4:(cb + 1) * 4, tok0:tok0 + n])], 1,
                          writes=[t_hB])
                    if fold:
                        R.op("dve", lambda e: e.tensor_tensor(out=he[:, :, 1:NHf], in0=hB[:, :, 1:NHf], in1=hB[:, :, n - 1:NHf:-1], op=ALU.add),
                             reads=[t_hB], writes=[t_he])
                        R.op("dve", lambda e: e.tensor_tensor(out=ho[:, :, 1:NHf], in0=hB[:, :, 1:NHf], in1=hB[:, :, n - 1:NHf:-1], op=ALU.subtract),
                             reads=[t_hB], writes=[t_ho])
                        R.op("act", lambda e: e.activation(out=he[:, :, 0:1], in_=hB[:, :, 0:1], func=AF.Copy), reads=[t_hB], writes=[t_he])
                        R.op("act", lambda e: e.activation(out=he[:, :, NHf:NHf + 1], in_=hB[:, :, NHf:NHf + 1], func=AF.Copy),
                             reads=[t_hB], writes=[t_he])

                def chan(cb):
                    for ntile in range(NCHn):
                        for gg in range(2):
                            pi = 1 + ((ntile * 2 + gg) % 2)
                            tsl = slice(ntile * P, (ntile + 1) * P)
                            if fold:
                                def fn(e, pi=pi, gg=gg, tsl=tsl):
                                    ins = None
                                    for half, srcb in ((0, he), (1, ho)):
                                        for j in range(2):
                                            ins = e.matmul(pss[pi][:, half * 256:(half + 1) * 256], lhsT=srcb[:, 2 * gg + j, tsl],
                                                           rhs=csc[:, j, half * 256:(half + 1) * 256], start=(j == 0), stop=(j == 1))
                                    return ins
                                R.op("pe", fn, reads=[t_he, t_ho, t_csc], writes=[t_ps[pi]])
                            else:
                                mm_group(pss[pi][:, :], [(hB[:, 2 * gg + j, tsl], csc[:, j, :]) for j in range(2)],
                                         reads=[t_hB, t_csc], tok_out=t_ps[pi])
                            if gg == 0:
                                R.op("act", lambda e, pi=pi, ntile=ntile, gg=gg: e.activation(
                                        out=PQ[:, ntile, :, gg, :], in_=pss[pi][:, :].rearrange("p (a b) -> p a b", a=2), func=AF.Copy),
                                     reads=[t_ps[pi]], writes=[t_PQ[ntile]])
                            else:
                                R.op("dve", lambda e, pi=pi, ntile=ntile, gg=gg: e.tensor_copy(
                                        out=PQ[:, ntile, :, gg, :], in_=pss[pi][:, :].rearrange("p (a b) -> p a b", a=2)),
                                     reads=[t_ps[pi]], writes=[t_PQ[ntile]])

                def pos(cb):
                    for nb in range(NB):
                        first = True
                        for cs in range(2):
                            for hf in range(NHALF):
                                buf, tk = ringF.get()
                                last = (cs == 1 and hf == NHALF - 1)
                                for ch in range(4):
                                    def fn(e, ch=ch, buf=buf, cs=cs, hf=hf, first=first, last=last):
                                        ins = None
                                        for k in range(KH):
                                            ins = e.matmul(pss[3 + ch][:, :BW], lhsT=PQf[:, hf * KH + k, cs, ch * P:(ch + 1) * P],
                                                           rhs=buf[:, k, :BW], start=(first and k == 0), stop=(last and k == KH - 1))
                                        return ins
                                    R.op("pe", fn, reads=[tk] + t_PQ[:NCHn], writes=[t_ps[3 + ch]])
                                first = False
                                ringF.release()
                        zb = zi[0] % 2
                        zi[0] += 1
                        for ch in range(4):
                            if ch % 2 == 0:
                                R.op("act", lambda e, ch=ch, zb=zb: e.activation(out=ztS[:, zb, ch, :BW], in_=pss[3 + ch][:, :BW], func=AF.Copy),
                                     reads=[t_ps[3 + ch]], writes=[t_zs[zb]])
                            else:
                                R.op("dve", lambda e, ch=ch, zb=zb: e.tensor_copy(out=ztS[:, zb, ch, :BW], in_=pss[3 + ch][:, :BW]),
                                     reads=[t_ps[3 + ch]], writes=[t_zs[zb]])
                        R.dma("sp", f"zs{zb}", lambda e, zb=zb, cb=cb, nb=nb: [e.dma_start(
                                out=fm(zt)[:, cb * 4:(cb + 1) * 4, tok0 + nb * BW:tok0 + (nb + 1) * BW], in_=ztS[:, zb, :, :BW])], 1,
                              reads=[t_zs[zb]])
                        for _ in range(2):
                            if ada_left:
                                s_ = ada_left.pop(0)
                                ada_slab(ringA2, 1, s_, 1 + (s_ % 2))

                prep(0)
                for cb in range(NCB):
                    chan(cb)
                    if cb + 1 < NCB:
                        prep(cb + 1)
                    pos(cb)

            argsS = (S, 0, dftS_d, KH_S, 1, NB_S, 512, True)
            argsL = (L, S, dftL_d, NCH_L, 1, 1, L, False)
            ringF.start(fourier(*argsS, seq_only=True) + fourier(*argsL, seq_only=True))
            fourier(*argsS)
            fourier(*argsL)
            while ada_left:
                s_ = ada_left.pop(0)
                ada_slab(ringA2, 1, s_, 1 + (s_ % 2))
            ada_coefs(1)
            R.barrier()
            R.flush()
        if upto <= 2:
            debug_tail(zt, bf=True)
            return nc

        with ExitStack() as st:
            alloc_common(st)
            ffn, ringB, aT, sg = ffn_phase(st, nB=3)
            xT_, hT_ = xT, hT
            rope = sb(st, "rope", [P, 2, T], F32)
            kv_alias = FC >= NH + DC + NKV + 4
            if kv_alias:
                o1 = NH + DC
                kS, t_kSl = aT[:, o1:o1 + NKV, :], t_a[o1:o1 + NKV]
                vS, t_vSl = aT[:, o1 + NKV:o1 + NKV + 4, :KVW], t_a[o1 + NKV:o1 + NKV + 4]
            else:
                kS = sb(st, "kS", [P, NKV, T], BF16)
                vS = sb(st, "vS", [P, 4, KVW], BF16)
                t_kSl, t_vSl = [Tok("kS")], [Tok("vS")]
            qn, t_qn = nt_, t_nt
            t1, t_t1 = sg, t_sg
            qS = aT[:, 0:NH, :].rearrange("p f t -> p (f t)").rearrange("p (k a g t) -> p k a g t", k=NKV, a=4, g=4)
            t_qS = t_a[0:NH]
            t_rope = Tok("rope")
            gl = groups()
            seqA, seqB = [], []
            for (g0, Tg, r) in gl:
                seqA += [c.a_wf + s for s in range(c.n_sq)]
                seqA += fin_seq(0, 1) + fin_seq(1, 0)
                if r == 0:
                    seqA += [c.a_qkv + s for s in range(c.n_sq + 2)]
                else:
                    seqA += [c.a_qkv + c.n_sq, c.a_qkv + c.n_sq + 1]
                seqB += fout_seq(0, 1) + fout_seq(1, 0)
            ringA.start(seqA)
            ringB.start(seqB)
            qt_v = qt.rearrange("p (k q g t) -> p k q g t", k=NKV, g=4, t=P)
            kt_v = kt.rearrange("p (k n) -> p k n", k=NKV)
            QB, NB_, RB = (1, 2), (0, 5), (7, 6)

            def qk_pipeline(jobs, Tg):
                cur = {}
                rstd_ = rstd

                def Pj(h):
                    jb = jobs[h]
                    if jb["first"]:
                        cur["buf"], cur["tk"] = ringA.get()
                    buf, tk, j = cur["buf"], cur["tk"], jb["j"]
                    pi = QB[h % 2]
                    mm_group(pss[pi][:, :Tg], [(buf[:, k, j * 128:(j + 1) * 128], hT_[:, k, :Tg]) for k in range(DC)],
                             reads=[tk] + t_h, tok_out=t_ps[pi])
                    if jb["last"]:
                        ringA.release()

                def Nj(h):
                    jb = jobs[h]
                    b = h % 2
                    pi = QB[b]
                    ps = pss[pi][:, :Tg]
                    rms_rstd([ps], [t_ps[pi]], Tg, 128, NB_[b], rb=b)
                    gi = jb["gi"]
                    if jb["rope"]:
                        R.op("dve", lambda e: e.scalar_tensor_tensor(out=qn[:, b, :Tg], in0=ps, scalar=qkg[:, gi:gi + 1],
                                                                     in1=rstd_[:, b, :Tg], op0=ALU.mult, op1=ALU.mult),
                             reads=[t_ps[pi], t_rstd[b], t_qkg], writes=[t_qn[b]])
                    else:
                        oc = jb["oc"]
                        R.op("dve", lambda e: e.scalar_tensor_tensor(out=kS[:, oc, :Tg], in0=ps, scalar=qkg[:, gi:gi + 1],
                                                                     in1=rstd_[:, b, :Tg], op0=ALU.mult, op1=ALU.mult),
                             reads=[t_ps[pi], t_rstd[b], t_qkg], writes=t_kSl)

                def Fj(h):
                    jb = jobs[h]
                    if not jb["rope"]:
                        return
                    b = h % 2
                    pr = RB[b]
                    R.op("pe", lambda e: e.matmul(pss[pr][:, :Tg], lhsT=rm[:], rhs=qn[:, b, :Tg], start=True, stop=True),
                         reads=[t_rm, t_qn[b]], writes=[t_ps[pr]])
                    R.op("dve", lambda e: e.tensor_tensor(out=t1[:, b, :Tg], in0=qn[:, b, :Tg], in1=rope[:, 0, :Tg], op=ALU.mult),
                         reads=[t_qn[b], t_rope], writes=[t_t1[b]])
                    R.op("dve", lambda e: e.tensor_tensor(out=qn[:, b, :Tg], in0=pss[pr][:, :Tg], in1=rope[:, 1, :Tg], op=ALU.mult),
                         reads=[t_ps[pr], t_rope, t_qn[b]], writes=[t_qn[b]])
                    oc = jb["oc"]
                    if jb["kind"] == "q":
                        kvh, g = oc // 4, oc % 4
                        R.op("dve", lambda e: e.tensor_tensor(out=qS[:, kvh, :, g, :],
                                                              in0=t1[:, b, :Tg].rearrange("p (a t) -> p a t", t=P),
                                                              in1=qn[:, b, :Tg].rearrange("p (a t) -> p a t", t=P), op=ALU.add),
                             reads=[t_t1[b], t_qn[b]], writes=t_qS)
                    else:
                        R.op("dve", lambda e: e.tensor_tensor(out=kS[:, oc, :Tg], in0=t1[:, b, :Tg], in1=qn[:, b, :Tg], op=ALU.add),
                             reads=[t_t1[b], t_qn[b]], writes=t_kSl)

                n = len(jobs)
                Pj(0)
                Nj(0)
                for h in range(1, n):
                    Pj(h)
                    Fj(h - 1)
                    Nj(h)
                Fj(n - 1)

            zpre = FC >= NH + DC
            zT = aT[:, NH:NH + DC, :] if zpre else hT_
            t_z = t_a[NH:NH + DC] if zpre else t_h

            def load_z(g0, Tg):
                R.dma("sp", "h", lambda e: [e.dma_start(out=zT[:, :, :Tg], in_=fm(zt)[:, :, g0:g0 + Tg])], 1, writes=t_z)

            if zpre:
                load_z(*gl[0][:2])
            for gi_, (g0, Tg, r) in enumerate(gl):
                if gi_ == 0:
                    load_x(xs, g0, Tg)
                if not zpre:
                    load_z(g0, Tg)
                if r == 0:
                    R.dma("sp", "rope", lambda e, g0=g0: [e.dma_start(
                            out=rope[:], in_=rope_d.rearrange("p (a n) -> p a n", a=2)[:, :, g0:g0 + T])], 1, writes=[t_rope])
                proj(c.n_sq, zT, t_z, Tg, resid_consume(0, 1, r, Tg))
                ffn(0, 1, r, Tg)
                ffn(1, 0, r, Tg)
                norm_mod(1, 1, r, Tg)
                store_x(xs, g0, Tg)
                if gi_ + 1 < len(gl):
                    load_x(xs, *gl[gi_ + 1][:2])
                    if zpre:
                        load_z(*gl[gi_ + 1][:2])
                jobs = []
                if r == 0:
                    for s in range(c.n_sq):
                        for j in range(4):
                            jobs.append(dict(first=(j == 0), last=(j == 3), j=j, gi=0, rope=True, oc=s * 4 + j, kind="q"))
                for j in range(NKV):
                    jobs.append(dict(first=(j == 0), last=(j == NKV - 1), j=j, gi=1, rope=(r == 0), oc=j, kind="k"))
                qk_pipeline(jobs, Tg)
                if r == 0:
                    R.dma("sp", "qS", lambda e, g0=g0: [e.dma_start(out=qt_v[:, :, g0 // P:g0 // P + 4, :, :], in_=qS)], 1,
                          reads=t_qS)
                R.dma("sp", "kS", lambda e, g0=g0, Tg=Tg: [e.dma_start(out=kt_v[:, :, g0:g0 + Tg], in_=kS[:, :, :Tg])], 1,
                      reads=t_kSl)
                buf, tk = ringA.get()
                for tt in range(Tg // P):
                    pi = 1 + (tt % 2)
                    mm_group(pss[pi][:, :KVW], [(hT_[:, k, tt * P:(tt + 1) * P], buf[:, k, :KVW]) for k in range(DC)],
                             reads=[tk] + t_h, tok_out=t_ps[pi])
                    if tt % 2 == 0:
                        R.op("act", lambda e, pi=pi, tt=tt: e.activation(out=vS[:, tt, :], in_=pss[pi][:, :KVW], func=AF.Copy),
                             reads=[t_ps[pi]], writes=t_vSl)
                    else:
                        R.op("dve", lambda e, pi=pi, tt=tt: e.tensor_copy(out=vS[:, tt, :], in_=pss[pi][:, :KVW]),
                             reads=[t_ps[pi]], writes=t_vSl)
                ringA.release()
                R.dma("sp", "vS", lambda e, g0=g0, Tg=Tg: [e.dma_start(
                        out=vv[g0:g0 + Tg, :].rearrange("(a p) f -> p a f", p=P), in_=vS[:, :Tg // P, :])], 1,
                      reads=t_vSl)
            R.barrier()
            R.flush()
        if upto <= 3:
            debug_tail(xs)
            return nc

        with ExitStack() as st:
            alloc_common(st, nA=2)
            xT_, hT_ = xT, hT
            NK = NT
            NKC = NK // P
            KT = sb(st, "KT", [P, NKV, NK], BF16)
            V = sb(st, "V", [P, NKC, KVW], BF16)
            qG = sb(st, "qG", [P, NKV, 4, 4 * P], BF16)
            pT = sb(st, "pT", [P, 4, 512], BF16)
            rden = sb(st, "rden", [P, 512], F32)
            accs = sb(st, "accs", [P, 512], F32)
            accb = sb(st, "accb", [P, 512], BF16)
            t_KT, t_V, t_qG, t_rden, t_accs, t_accb = Tok("KT"), Tok("V"), Tok("qG"), Tok("rden"), Tok("accs"), Tok("accb")
            t_pT = [Tok(f"pT{i}") for i in range(4)]
            R.dma("sp", "kv", lambda e: [e.dma_start(out=KT[:], in_=kt.rearrange("p (k n) -> p k n", k=NKV)),
                                         e.dma_start(out=V[:], in_=vv.rearrange("(a p) f -> p a f", p=P))], 2,
                  writes=[t_KT, t_V])
            gl = groups(False)
            ringA.start([c.a_wo + s for _ in gl for s in range(c.n_sq)])
            qt_g = qt.rearrange("p (k q f) -> p k q f", k=NKV, f=4 * P)
            scale = 1.0 / math.sqrt(128.0)
            SB = (1, 2, 7)
            blk = 0
            pending = [None]
            bg_jobs = [("A", i_) for i_ in fin_seq(1, 1)] + [("B", i_) for i_ in fout_seq(1, 1)]
            n_blocks = len(gl) * NKV * 4
            bg_per_blk = -(-len(bg_jobs) // max(1, n_blocks - 1))

            def bg_convert(kind, idx):
                if kind == "A":
                    s_ = WA[idx * P:(idx + 1) * P, :].rearrange("p (a b) -> p a b", a=nsubA)
                    d_ = WAc[(idx - a0) * P:(idx - a0 + 1) * P, :].rearrange("p (a b) -> p a b", a=nsubA)
                    ch = cacheA
                else:
                    s_ = WB[idx * P:(idx + 1) * P, :].rearrange("p (a b) -> p a b", a=nsubB)
                    d_ = WBc[idx * P:(idx + 1) * P, :].rearrange("p (a b) -> p a b", a=nsubB)
                    ch = cacheB
                if idx in ch["seen"]:
                    return
                R.dma("pool", "bgc", lambda e: [e.dma_start(out=d_, in_=s_)], 1, writes=[ch["tok"]])
                ch["seen"].add(idx)

            def tail2(args):
                po, kvh, tt, st0 = args
                R.op("pe", lambda e: e.matmul(pss[0][:, :], lhsT=ones[:], rhs=accb[:], start=st0, stop=True),
                     reads=[t_accb, t_ones], writes=[t_ps[0]])
                R.op("act", lambda e: e.activation(out=rden[:], in_=pss[0][:, :], func=AF.Ln),
                     reads=[t_ps[0]], writes=[t_rden])
                R.op("act", lambda e: e.activation(out=rden[:], in_=rden[:], func=AF.Exp, scale=-1.0),
                     reads=[t_rden], writes=[t_rden])
                R.op("dve", lambda e: e.tensor_tensor(
                        out=hT_[:, 4 * kvh:4 * kvh + 4, tt * P:(tt + 1) * P],
                        in0=pss[po][:, :].rearrange("p (g t) -> p g t", g=4),
                        in1=rden.rearrange("p (g t) -> p g t", g=4), op=ALU.mult),
                     reads=[t_ps[po], t_rden], writes=t_h[4 * kvh:4 * kvh + 4])

            for (g0, Tg, r) in gl:
                R.dma("sp", "qG", lambda e, g0=g0: [e.dma_start(out=qG[:], in_=qt_g[:, :, g0 // P:g0 // P + 4, :])], 1,
                      writes=[t_qG])
                load_x(xs, g0, Tg)
                for kvh in range(NKV):
                    for tt in range(4):
                        po = 3 + (blk % 2)
                        blk += 1
                        qb = qG[:, kvh, tt, :]

                        def s_mm(kc, kvh=kvh, qb=qb):
                            pi = SB[kc % 3]
                            R.op("pe", lambda e: e.matmul(pss[pi][:, :], lhsT=KT[:, kvh, kc * P:(kc + 1) * P], rhs=qb,
                                                          start=True, stop=True),
                                 reads=[t_KT, t_qG], writes=[t_ps[pi]])
                        s_mm(0)
                        if NKC > 1:
                            s_mm(1)
                        pe_den = [kc for kc in range(NKC) if kc % 4 == 3]
                        dve_cnt = 0
                        for kc in range(NKC):
                            if kc + 2 < NKC:
                                s_mm(kc + 2)
                            pi = SB[kc % 3]
                            pb = kc % 4
                            R.op("act", lambda e, pi=pi, pb=pb: e.activation(out=pT[:, pb, :], in_=pss[pi][:, :], func=AF.Exp, scale=scale),
                                 reads=[t_ps[pi]], writes=[t_pT[pb]])
                            on_pe = kc in pe_den

                            def pv(e, kc=kc, kvh=kvh, pb=pb, po=po, on_pe=on_pe):
                                ins = e.matmul(pss[po][:, :], lhsT=V[:, kc, kvh * P:(kvh + 1) * P], rhs=pT[:, pb, :],
                                               start=(kc == 0), stop=(kc == NKC - 1))
                                if on_pe:
                                    ins = e.matmul(pss[0][:, :], lhsT=ones[:], rhs=pT[:, pb, :],
                                                   start=(kc == pe_den[0]), stop=False)
                                return ins
                            R.op("pe", pv, reads=[t_V, t_pT[pb], t_ones], writes=[t_ps[po]] + ([t_ps[0]] if on_pe else []))
                            if not on_pe:
                                a = 5 + (dve_cnt % 2)
                                if dve_cnt < 2:
                                    R.op("dve", lambda e, a=a, pb=pb: e.tensor_copy(out=pss[a][:, :], in_=pT[:, pb, :]),
                                         reads=[t_pT[pb]], writes=[t_ps[a]])
                                else:
                                    R.op("dve", lambda e, a=a, pb=pb: e.tensor_tensor(out=pss[a][:, :], in0=pss[a][:, :], in1=pT[:, pb, :], op=ALU.add),
                                         reads=[t_pT[pb], t_ps[a]], writes=[t_ps[a]])
                                dve_cnt += 1
                            if kc == 1 and pending[0] is not None:
                                tail2(pending[0])
                                pending[0] = None
                        if dve_cnt >= 2:
                            R.op("dve", lambda e: e.tensor_copy(out=accs[:], in_=pss[6][:, :]), reads=[t_ps[6]], writes=[t_accs])
                            R.op("dve", lambda e: e.tensor_tensor(out=accb[:], in0=pss[5][:, :], in1=accs[:], op=ALU.add),
                                 reads=[t_ps[5], t_accs], writes=[t_accb])
                        else:
                            R.op("dve", lambda e: e.tensor_copy(out=accb[:], in_=pss[5][:, :]), reads=[t_ps[5]], writes=[t_accb])
                        assert pending[0] is None
                        pending[0] = (po, kvh, tt, len(pe_den) == 0)
                        for _ in range(bg_per_blk):
                            if bg_jobs:
                                bg_convert(*bg_jobs.pop(0))
                tail2(pending[0])
                pending[0] = None
                proj(c.n_sq, hT_, t_h, Tg, resid_consume(1, 1, 0, Tg), ps_ids=(0, 7))
                store_x(xs, g0, Tg)
            R.barrier()
            R.flush()
        if upto <= 4:
            debug_tail(xs)
            return nc

        with ExitStack() as st:
            alloc_common(st)
            ffn, ringB, aT, sg = ffn_phase(st, nB=3)
            gl = groups(False)
            ringA.start([s for _ in gl for s in fin_seq(1, 1)])
            ringB.start([s for _ in gl for s in fout_seq(1, 1)])
            for (g0, Tg, r) in gl:
                load_x(xs, g0, Tg)
                ffn(1, 1, 0, Tg)
                store_x(outT, g0, Tg)
            R.barrier()
            R.flush()
    return nc


def _fm_vec(v):
    v = np.asarray(v, np.float32)
    sh = v.shape[:-1]
    dc = v.shape[-1] // P
    return np.moveaxis(v.reshape(*sh, dc, P), -1, 0)


def _slabA(w):
    din, dout = w.shape
    dc = din // P
    return np.ascontiguousarray(w.reshape(dc, P, dout // 512, 512).transpose(2, 1, 0, 3))


def host_consts(c):
    bf = ml_dtypes.bfloat16
    out = {}
    a = np.arange(256)
    ang = 2.0 * np.pi * ((a[:, None] * a[None, :]) % 256) / 256.0
    cs = np.concatenate([np.cos(ang), -np.sin(ang)], axis=1) / 16.0
    out["csc"] = np.ascontiguousarray(cs.reshape(2, P, 512).transpose(1, 0, 2).reshape(P, 1024)).astype(bf)

    def dft_tab(n, KH, BW, fold=False):
        a = np.arange(n, dtype=np.int64)
        ang = 2.0 * np.pi * ((a[:, None] * a[None, :]) % n).astype(np.float64) / n
        tabs = np.stack([np.cos(ang), np.sin(ang)], 0) / math.sqrt(n)
        if fold:
            t2 = np.zeros((2, KH * P, n))
            t2[:, :n // 2 + 1] = tabs[:, :n // 2 + 1]
            tabs = t2
        NHALF = tabs.shape[1] // P // KH
        NB = n // BW
        t = tabs.reshape(2, NHALF, KH, P, NB, BW).transpose(4, 0, 1, 3, 2, 5)
        return np.ascontiguousarray(t).reshape(NB * 2 * NHALF * P, KH * BW).astype(bf)

    KH_S = (c.S // 2) // P + 1
    out["dftS"] = dft_tab(c.S, KH_S, 512, fold=True)
    out["dftL"] = dft_tab(c.L, c.L // P, c.L)
    rf = 32
    t = np.arange(c.S)
    row = (t // c.GRID_W).astype(np.float32)
    col = (t % c.GRID_W).astype(np.float32)
    inv = (np.float32(c.THETA) ** (-np.arange(rf, dtype=np.float32) / np.float32(rf))).astype(np.float32)
    a_r = row[:, None] * inv
    a_c = col[:, None] * inv
    ang = np.concatenate([a_r, a_r, a_c, a_c], axis=-1)
    out["rope"] = np.ascontiguousarray(np.stack([np.cos(ang).T, np.sin(ang).T], 1).reshape(P, 2 * c.S)).astype(np.float32)
    rm = np.zeros((P, P), np.float32)
    for j in range(P):
        if (j // 32) % 2 == 0:
            rm[j + 32, j] = -1.0
        else:
            rm[j - 32, j] = 1.0
    out["rm"] = rm
    return out


def host_weights(c, w_ada, b_ada, norm_g, w_ffn_in, w_ffn_out, w_fourier_out, w_qkv, q_norm_g, k_norm_g, w_attn_out, c_ctx):
    D, FF = c.D, c.FF
    slabs = []
    for l in range(2):
        slabs.append(_slabA(np.asarray(w_ada[l], np.float32)))
    for l in range(2):
        for i in range(2):
            w = np.asarray(w_ffn_in[l, i], np.float32)
            g = w[:, :FF].reshape(c.DC, P, c.n_fin, 256)
            u = w[:, FF:].reshape(c.DC, P, c.n_fin, 256)
            s = np.concatenate([g, u], axis=-1).transpose(2, 1, 0, 3)
            slabs.append(np.ascontiguousarray(s))
    slabs.append(_slabA(np.asarray(w_fourier_out[0], np.float32)))
    wq = np.asarray(w_qkv[0], np.float32)
    kvw = c.NKV * P
    slabs.append(_slabA(wq[:, :D]))
    for part in (wq[:, D:D + kvw], wq[:, D + kvw:D + 2 * kvw]):
        pad = np.zeros((D, 512), np.float32)
        pad[:, :kvw] = part
        slabs.append(_slabA(pad))
    slabs.append(_slabA(np.asarray(w_attn_out[0], np.float32)))
    WA = np.concatenate(slabs, axis=0)
    assert WA.shape[0] == c.n_a, (WA.shape, c.n_a)
    WA = WA.reshape(c.n_a * P, c.DC * 512)
    wb = []
    for l in range(2):
        for i in range(2):
            w = np.asarray(w_ffn_out[l, i], np.float32)
            wb.append(np.ascontiguousarray(w.reshape(c.FC, P, c.DC, P).transpose(2, 1, 0, 3)))
    WB = np.concatenate(wb, axis=0).reshape(c.n_b * P, c.FC * P)
    bd = _fm_vec(np.asarray(b_ada, np.float32).reshape(2, 9, D))
    bada = np.ascontiguousarray(np.repeat(bd[..., None], 2, axis=-1)).reshape(P, -1)
    ngv = np.ascontiguousarray(_fm_vec(np.asarray(norm_g, np.float32))).reshape(P, -1)
    qkg = np.ascontiguousarray(np.stack([np.asarray(q_norm_g, np.float32)[0], np.asarray(k_norm_g, np.float32)[0]], axis=1))
    return dict(WA=WA, WB=WB, bada=bada, ng=ngv, qkg=qkg)


_CACHE = {}


def run(cfg, x, c, ctx, c_ctx, w_ada, b_ada, norm_g, w_ffn_in, w_ffn_out, w_fourier_out, w_qkv,
        q_norm_g, k_norm_g, w_attn_out, upto=99, cores=None):
    B = x.shape[0]
    cores = list(range(B)) if cores is None else cores
    shared = host_consts(cfg)
    shared.update(host_weights(cfg, w_ada, b_ada, norm_g, w_ffn_in, w_ffn_out, w_fourier_out, w_qkv,
                               q_norm_g, k_norm_g, w_attn_out, c_ctx))
    x = np.asarray(x, np.float32)
    ctx = np.asarray(ctx, np.float32)
    cv = np.asarray(c, np.float32)
    ccx = np.asarray(c_ctx, np.float32)
    in_maps = []
    for b in cores:
        m = dict(shared)
        m["xT0"] = np.ascontiguousarray(np.concatenate([x[b].T, ctx[b].T], axis=1))
        m["cc"] = np.ascontiguousarray(np.stack([_fm_vec(cv[b]), _fm_vec(ccx)], axis=-1)).reshape(P, -1)
        in_maps.append(m)
    key = (cfg.D, cfg.FF, cfg.S, cfg.L, upto)
    if key not in _CACHE:
        _CACHE[key] = build(cfg, upto)
    nc = _CACHE[key]
    res = run_bass_kernel_spmd(nc, in_maps, core_ids=list(range(len(cores))))
    out = np.stack([np.ascontiguousarray(r["outT"].T) for r in res.results], axis=0)
    return out.astype(np.float32)


def kernel(x, c, ctx, c_ctx, w_ada, b_ada, norm_g, w_ffn_in, w_ffn_out, w_fourier_out, w_qkv,
           q_norm_g, k_norm_g, w_attn_out):
    cfg = Cfg()
    return run(cfg, x, c, ctx, c_ctx, w_ada, b_ada, norm_g, w_ffn_in, w_ffn_out, w_fourier_out, w_qkv,
               q_norm_g, k_norm_g, w_attn_out)
```

```python
import math
from contextlib import ExitStack

import numpy as np
import ml_dtypes

import concourse.bass as bass
import concourse.mybir as mybir
from concourse.bass_utils import run_bass_kernel_spmd

F32 = mybir.dt.float32
BF16 = mybir.dt.bfloat16
AF = mybir.ActivationFunctionType
ALU = mybir.AluOpType
EPS = 1e-6
P = 128


class Cfg:
    def __init__(self, D=2048, FF=5632, S=4096, L=256, NH=16, NKV=4, FG=8, T=512,
                 GRID_W=64, THETA=10000.0):
        self.D, self.FF, self.S, self.L, self.NH, self.NKV, self.FG, self.T = D, FF, S, L, NH, NKV, FG, T
        self.GRID_W, self.THETA = GRID_W, THETA
        self.HD = 128
        self.DC = D // P
        self.FC = FF // P
        self.G = NH // NKV
        assert self.G == 4 and NH * 128 == D and D % 512 == 0 and self.FC % 2 == 0
        self.FGD = D // FG
        assert self.FGD == 256
        self.NT = S + L
        self.NG = S // T
        assert S % T == 0 and T == 512 and L % 128 == 0 and L <= 512
        self.NMOD = 9
        self.n_ada = 9 * D // 512
        self.n_fin = self.FC // 2
        self.n_sq = D // 512
        self.n_qkv = D // 512 + 2
        off = 0
        self.a_ada = [off, off + self.n_ada]; off += 2 * self.n_ada
        self.a_fin = {}
        for l in range(2):
            for i in range(2):
                self.a_fin[(l, i)] = off; off += self.n_fin
        self.a_wf = off; off += self.n_sq
        self.a_qkv = off; off += self.n_qkv
        self.a_wo = off; off += self.n_sq
        self.n_a = off
        self.b_fout = {}
        off = 0
        for l in range(2):
            for i in range(2):
                self.b_fout[(l, i)] = off; off += self.DC
        self.n_b = off


class Tok:
    __slots__ = ("name", "w", "rs")

    def __init__(self, name):
        self.name = name
        self.w = None
        self.rs = {}


class Op:
    __slots__ = ("eng", "fn", "waits", "sigval", "dma_key", "dma_val", "is_dma")


ENGS = ("pe", "act", "dve", "pool", "sp")


class Rec:
    def __init__(self, nc, es):
        self.nc = nc
        self.es = es
        self.sems = {}
        self.cnt = {}
        self.ops = {e: [] for e in ENGS}
        self.waited = {e: {} for e in ENGS}
        for e in ENGS:
            self._sem("E_" + e)

    def _sem(self, key):
        if key not in self.sems:
            self.sems[key] = self.es.enter_context(self.nc.semaphore(key))
            self.cnt[key] = 0
        return self.sems[key]

    def _add_wait(self, op, key, val):
        w = self.waited[op.eng]
        if w.get(key, 0) >= val:
            return
        w[key] = val
        op.waits.append((key, val))

    def _dep(self, op, d, same_ok=False):
        if d is None:
            return
        if d.is_dma:
            self._add_wait(op, d.dma_key, self.cnt[d.dma_key])
        else:
            if d.eng == op.eng and not op.is_dma and (same_ok or op.eng == "pe"):
                return
            self._add_wait(op, "E_" + d.eng, d.sigval)

    def _mk(self, eng, fn, reads, writes, is_dma):
        op = Op()
        op.eng, op.fn, op.waits, op.is_dma = eng, fn, [], is_dma
        op.sigval = op.dma_key = op.dma_val = None
        for t in reads:
            self._dep(op, t.w)
        for t in writes:
            self._dep(op, t.w)
            for r in t.rs.values():
                self._dep(op, r, same_ok=True)
        for t in writes:
            t.w = op
            t.rs = {}
        for t in reads:
            t.rs[eng if not is_dma else ("dma", id(op))] = op
        self.ops[eng].append(op)
        return op

    def op(self, eng, fn, reads=(), writes=()):
        o = self._mk(eng, fn, reads, writes, False)
        key = "E_" + eng
        self.cnt[key] += 1
        o.sigval = self.cnt[key]
        return o

    def dma(self, queue, key, fn, n, reads=(), writes=()):
        key = "D_" + key
        self._sem(key)
        o = self._mk(queue, fn, reads, writes, True)
        o.dma_key = key
        self.cnt[key] += 16 * n
        o.dma_val = n
        return o

    def barrier(self):
        for e in ENGS:
            o = Op()
            o.eng, o.fn, o.waits, o.is_dma = e, None, [], False
            o.sigval = o.dma_key = o.dma_val = None
            for key, c in self.cnt.items():
                if c > 0 and key != "E_" + e:
                    self._add_wait(o, key, c)
            self.ops[e].append(o)

    def flush(self):
        nc = self.nc
        ops = self.ops
        sems = self.sems
        self.ops = {e: [] for e in ENGS}

        def run(eng, lst, ekey):
            sem_e = sems[ekey]
            for o in lst:
                for key, val in o.waits:
                    eng.wait_ge(sems[key], val)
                if o.fn is None:
                    continue
                if o.is_dma:
                    ins = o.fn(eng)
                    assert len(ins) == o.dma_val, (len(ins), o.dma_val)
                    s = sems[o.dma_key]
                    for i in ins:
                        i.then_inc(s, 16)
                else:
                    i = o.fn(eng)
                    i.then_inc(sem_e, 1)

        with nc.Block() as block:
            @block.tensor
            def _(e):
                run(e, ops["pe"], "E_pe")

            @block.scalar
            def _(e):
                run(e, ops["act"], "E_act")

            @block.vector
            def _(e):
                run(e, ops["dve"], "E_dve")

            @block.gpsimd
            def _(e):
                run(e, ops["pool"], "E_pool")

            @block.sync
            def _(e):
                run(e, ops["sp"], "E_sp")


class Ring:
    def __init__(self, rec, name, bufs, loader, queue, cache=None):
        self.rec, self.name, self.bufs, self.loader, self.queue = rec, name, bufs, loader, queue
        self.cache = cache
        self.n = len(bufs)
        self.toks = [Tok(f"{name}{i}") for i in range(self.n)]
        self.seq = []
        self.head = 0
        self.loaded = 0

    def start(self, seq):
        assert self.head == len(self.seq), (self.name, self.head, len(self.seq))
        self.seq = list(seq)
        self.head = 0
        self.loaded = 0
        for _ in range(min(self.n, len(self.seq))):
            self._load()

    def _load(self):
        i = self.loaded
        slot = i % self.n
        src = self.seq[i]
        buf = self.bufs[slot]
        ch = self.cache
        if ch is not None and isinstance(src, int) and src >= ch["base"]:
            if src in ch["seen"]:
                fn, n = ch["load_bf"](buf, src)
                self.rec.dma(self.queue, f"{self.name}{slot}", fn, n, reads=[ch["tok"]], writes=[self.toks[slot]])
            else:
                fn, n = self.loader(buf, src)
                self.rec.dma(self.queue, f"{self.name}{slot}", fn, n, writes=[self.toks[slot]])
                v = ch["visits"].get(src, 0)
                ch["visits"][src] = v + 1
                if v == src % 3:
                    fn2, n2 = ch["store"](buf, src)
                    self.rec.dma("sp", f"{self.name}wb", fn2, n2, reads=[self.toks[slot]], writes=[ch["tok"]])
                    ch["seen"].add(src)
        else:
            fn, n = self.loader(buf, src)
            self.rec.dma(self.queue, f"{self.name}{slot}", fn, n, writes=[self.toks[slot]])
        self.loaded += 1

    def get(self):
        slot = self.head % self.n
        return self.bufs[slot], self.toks[slot]

    def release(self):
        self.head += 1
        if self.loaded < len(self.seq):
            self._load()


def _split(n, cap=2048):
    a = 1
    while n % a or n // a > cap:
        a += 1
    return a


def build(cfg, upto=99):
    c = cfg
    D, FF, S, L, NT, T, DC, FC = c.D, c.FF, c.S, c.L, c.NT, c.T, c.DC, c.FC
    NH, NKV = c.NH, c.NKV
    KVW = NKV * P
    nc = bass.Bass("TRN2", target_bir_lowering=False)

    def din(name, shape, dt):
        return nc.dram_tensor(name, list(shape), dt, kind="ExternalInput").ap()

    xT0 = din("xT0", [D, NT], F32)
    WA = din("WA", [c.n_a * P, DC * 512], F32)
    WB = din("WB", [c.n_b * P, FC * 128], F32)
    cc_d = din("cc", [P, DC * 2], F32)
    bada_d = din("bada", [P, 2 * 9 * DC * 2], F32)
    ng_d = din("ng", [P, 2 * 3 * DC], F32)
    qkg_d = din("qkg", [P, 2], F32)
    csc_d = din("csc", [P, 2 * 512], BF16)
    rope_d = din("rope", [P, 2 * S], F32)
    rm_d = din("rm", [P, P], F32)
    NCH_S = S // P
    KH_S = (S // 2) // P + 1
    NHALF_S = 1
    NB_S = S // 512
    dftS_d = din("dftS", [NB_S * 2 * NHALF_S * P, KH_S * 512], BF16)
    NCH_L = L // P
    dftL_d = din("dftL", [2 * P, NCH_L * L], BF16)
    outT = nc.dram_tensor("outT", [D, S], F32, kind="ExternalOutput").ap()
    xs = nc.dram_tensor("xs", [D, NT], F32).ap()
    h1 = nc.dram_tensor("h1", [D, NT], BF16).ap()
    zt = nc.dram_tensor("zt", [D, NT], BF16).ap()
    qt = nc.dram_tensor("qt", [P, NKV * (S // P) * 4 * P], BF16).ap()
    kt = nc.dram_tensor("kt", [P, NKV * NT], BF16).ap()
    vv = nc.dram_tensor("vv", [NT, KVW], BF16).ap()
    a0 = c.a_fin[(0, 0)]
    WAc = nc.dram_tensor("WAc", [(c.n_a - a0) * P, DC * 512], BF16).ap()
    WBc = nc.dram_tensor("WBc", [c.n_b * P, FC * 128], BF16).ap()

    def fm(ap):
        return ap.rearrange("(k p) n -> p k n", p=P)

    es = ExitStack()
    with es:
        R = Rec(nc, es)

        uid = [0]

        def sb(st, name, shape, dt):
            uid[0] += 1
            return st.enter_context(nc.sbuf_tensor(f"s{uid[0]}_{name}", list(shape), dt))

        ones = sb(es, "ones", [P, P], BF16)
        sT = sb(es, "sT", [P, DC, 2], BF16)
        ccs = sb(es, "ccs", [P, DC, 2], F32)
        mod = sb(es, "mod", [P, 2, 9, DC, 2], F32)
        bada = sb(es, "bada", [P, 2, 9, DC, 2], F32)
        ng = sb(es, "ng", [P, 2, 3, DC], F32)
        coefA = sb(es, "coefA", [P, 2, 3, 2, DC], F32)
        coefB = sb(es, "coefB", [P, 2, 3, 2, DC], F32)
        coefG = sb(es, "coefG", [P, 2, 3, 2, DC], F32)
        qkg = sb(es, "qkg", [P, 2], F32)
        rm = sb(es, "rm", [P, P], F32)
        epsc = sb(es, "epsc", [P, 1], F32)
        pss = [es.enter_context(nc.psum_tensor(f"ps{i}", [P, 512], F32)) for i in range(8)]

        t_ones, t_sT, t_cc, t_mod, t_bada, t_ng, t_coef, t_qkg, t_rm = (Tok(n) for n in
            ("ones", "sT", "cc", "mod", "bada", "ng", "coef", "qkg", "rm"))
        t_x = [Tok(f"x{k}") for k in range(DC)]
        t_h = [Tok(f"h{k}") for k in range(DC)]
        t_sq = [Tok("sq0"), Tok("sq1")]
        t_nt = [Tok("nt0"), Tok("nt1")]
        t_rstd = [Tok("rstd0"), Tok("rstd1")]
        t_rscr = [Tok("rscr0"), Tok("rscr1")]
        t_ps = [Tok(f"ps{i}") for i in range(8)]

        xT = hT = wA_t = sq = nt_ = rstd = ringA = rscr = None
        nsubA = _split(DC * 512)

        def loadA(buf, idx):
            def fn(e):
                src = WA[idx * P:(idx + 1) * P, :].rearrange("p (a b) -> p a b", a=nsubA)
                dst = buf.rearrange("p k n -> p (k n)").rearrange("p (a b) -> p a b", a=nsubA)
                return [e.dma_start(out=dst, in_=src)]
            return fn, 1

        cacheA = dict(base=a0, seen=set(), visits={}, tok=Tok("cacheA"),
                      load_bf=lambda buf, idx: (lambda e: [e.dma_start(
                          out=buf.rearrange("p k n -> p (k n)"), in_=WAc[(idx - a0) * P:(idx - a0 + 1) * P, :])], 1),
                      store=lambda buf, idx: (lambda e: [e.dma_start(
                          out=WAc[(idx - a0) * P:(idx - a0 + 1) * P, :], in_=buf.rearrange("p k n -> p (k n)"))], 1))
        cacheB = dict(base=0, seen=set(), visits={}, tok=Tok("cacheB"),
                      load_bf=lambda buf, idx: (lambda e: [e.dma_start(
                          out=buf.rearrange("p k n -> p (k n)"), in_=WBc[idx * P:(idx + 1) * P, :])], 1),
                      store=lambda buf, idx: (lambda e: [e.dma_start(
                          out=WBc[idx * P:(idx + 1) * P, :], in_=buf.rearrange("p k n -> p (k n)"))], 1))

        def alloc_common(st, with_x=True, nA=3):
            nonlocal xT, hT, wA_t, sq, nt_, rstd, ringA, rscr
            if with_x:
                xT = sb(st, "xT", [P, DC, T], F32)
                hT = sb(st, "hT", [P, DC, T], BF16)
            wA_t = sb(st, "wA", [P, nA, DC, 512], BF16)
            sq = sb(st, "sq", [P, 2, T], BF16)
            nt_ = sb(st, "nt", [P, 2, T], F32)
            rstd = sb(st, "rstd", [P, 2, T], F32)
            rscr = sb(st, "rscr", [P, 2, T], F32)
            ringA = Ring(R, "wA", [wA_t[:, i] for i in range(nA)], loadA, "pool", cache=cacheA)

        def vec(tile, l, i, r, k):
            return tile[:, l, i, r, k:k + 1]

        def mm_group(out_ap, pairs, reads, tok_out):
            n = len(pairs)

            def fn(e):
                ins = None
                for j, (a, b) in enumerate(pairs):
                    ins = e.matmul(out_ap, lhsT=a, rhs=b, start=(j == 0), stop=(j == n - 1))
                return ins
            return R.op("pe", fn, reads=reads, writes=[tok_out])

        def rms_rstd(src_chunks, src_toks, Tg, nfeat, ps_i, rb=0, out_ps=None):
            n = len(src_chunks)
            psn = pss[ps_i][:, :Tg]
            sq_, rstd_, rscr_ = sq, rstd, rscr
            for k in range(n):
                b = (k + rb) % 2
                R.op("act", lambda e, k=k, b=b: e.activation(out=sq_[:, b, :Tg], in_=src_chunks[k], func=AF.Square),
                     reads=[src_toks[k]], writes=[t_sq[b]])
                R.op("pe", lambda e, k=k, b=b: e.matmul(psn, lhsT=ones[:], rhs=sq_[:, b, :Tg],
                                                       start=(k == 0), stop=(k == n - 1)),
                     reads=[t_sq[b], t_ones], writes=[t_ps[ps_i]])
            R.op("act", lambda e: e.activation(out=rscr_[:, rb, :Tg], in_=psn, func=AF.Ln,
                                               bias=epsc[:, 0:1], scale=1.0 / nfeat),
                 reads=[t_ps[ps_i]], writes=[t_rscr[rb]])
            if out_ps is None:
                R.op("act", lambda e: e.activation(out=rstd_[:, rb, :Tg], in_=rscr_[:, rb, :Tg], func=AF.Exp, scale=-0.5),
                     reads=[t_rscr[rb]], writes=[t_rstd[rb]])
            else:
                R.op("act", lambda e: e.activation(out=pss[out_ps][:, :Tg], in_=rscr_[:, rb, :Tg], func=AF.Exp, scale=-0.5),
                     reads=[t_rscr[rb]], writes=[t_ps[out_ps]])

        def norm_mod(l, i, r, Tg):
            xT_, hT_, nt2, rstd_ = xT, hT, nt_, rstd
            rms_rstd([xT_[:, k, :Tg] for k in range(DC)], t_x, Tg, D, 0, out_ps=7)
            for k in range(DC):
                b = k % 2
                R.op("dve", lambda e, k=k, b=b: e.tensor_tensor(out=nt2[:, b, :Tg], in0=xT_[:, k, :Tg],
                                                                in1=pss[7][:, :Tg], op=ALU.mult),
                     reads=[t_x[k], t_ps[7]], writes=[t_nt[b]])
                R.op("act", lambda e, k=k, b=b: e.activation(out=hT_[:, k, :Tg], in_=nt2[:, b, :Tg], func=AF.Identity,
                                                             scale=vec(coefA, l, i, r, k), bias=vec(coefB, l, i, r, k)),
                     reads=[t_nt[b], t_coef], writes=[t_h[k]])

        def proj(nslab, inT, in_toks, Tg, consume, ps_ids=(1, 2), jmax=4):
            for s in range(nslab):
                buf, tk = ringA.get()
                for j in range(jmax):
                    oc = s * 4 + j
                    pi = ps_ids[oc % len(ps_ids)]
                    mm_group(pss[pi][:, :Tg],
                             [(buf[:, k, j * 128:(j + 1) * 128], inT[:, k, :Tg]) for k in range(DC)],
                             reads=[tk] + list(in_toks), tok_out=t_ps[pi])
                    consume(oc, pss[pi][:, :Tg], t_ps[pi])
                ringA.release()

        def resid_consume(l, i, r, Tg):
            xT_ = xT

            def consume(oc, ps, ptk):
                R.op("dve", lambda e: e.scalar_tensor_tensor(out=xT_[:, oc, :Tg], in0=ps, scalar=vec(coefG, l, i, r, oc),
                                                             in1=xT_[:, oc, :Tg], op0=ALU.mult, op1=ALU.add),
                     reads=[ptk, t_x[oc], t_coef], writes=[t_x[oc]])
            return consume

        NQ = 4 if DC % 4 == 0 else 1
        QC = DC // NQ

        def load_x(src, g0, Tg):
            xT_ = xT
            for q in range(NQ):
                ks = slice(q * QC, (q + 1) * QC)
                R.dma("sp", f"x{q}", lambda e, ks=ks: [e.dma_start(out=xT_[:, ks, :Tg], in_=fm(src)[:, ks, g0:g0 + Tg])], 1,
                      writes=t_x[ks])

        def store_x(dst, g0, Tg):
            xT_ = xT
            for q in range(NQ):
                ks = slice(q * QC, (q + 1) * QC)
                R.dma("sp", f"x{q}", lambda e, ks=ks: [e.dma_start(out=fm(dst)[:, ks, g0:g0 + Tg], in_=xT_[:, ks, :Tg])], 1,
                      reads=t_x[ks])

        def groups(with_ctx=True):
            gl = [(g * T, T, 0) for g in range(c.NG)]
            if with_ctx:
                gl.append((S, L, 1))
            return gl

        def debug_tail(src, bf=False):
            with ExitStack() as st:
                alloc_common(st)
                xT_, hT_ = xT, hT
                for g in range(S // T):
                    g0 = g * T
                    if bf:
                        R.dma("sp", "h", lambda e, g0=g0: [e.dma_start(out=hT_[:, :, :], in_=fm(src)[:, :, g0:g0 + T])], 1, writes=t_h)
                        for k in range(DC):
                            R.op("dve", lambda e, k=k: e.tensor_copy(out=xT_[:, k, :], in_=hT_[:, k, :]), reads=[t_h[k]], writes=[t_x[k]])
                    else:
                        load_x(src, g0, T)
                    store_x(outT, g0, T)
                R.barrier()
                R.flush()

        modf = mod.rearrange("p l j k r -> p (l j k) r")
        badaf = bada.rearrange("p l j k r -> p (l j k) r")

        def ada_slab(ring, l, s, pi):
            buf, tk = ring.get()
            for j in range(4):
                mm_group(pss[pi][:, 2 * j:2 * j + 2],
                         [(buf[:, k, j * 128:(j + 1) * 128], sT[:, k, :]) for k in range(DC)],
                         reads=[tk, t_sT], tok_out=t_ps[pi])
            c0 = (l * c.n_ada + s) * 4
            R.op("dve", lambda e: e.tensor_tensor(
                    out=modf[:, c0:c0 + 4, :], in0=pss[pi][:, 0:8].rearrange("p (j r) -> p j r", r=2),
                    in1=badaf[:, c0:c0 + 4, :], op=ALU.add),
                 reads=[t_ps[pi], t_bada], writes=[t_mod])
            ring.release()

        def ada_coefs(l):
            for i in range(3):
                for r in range(2):
                    R.op("dve", lambda e, i=i, r=r: e.scalar_tensor_tensor(
                            out=coefA[:, l, i, r, :], in0=mod[:, l, 3 * i + 1, :, r], scalar=1.0,
                            in1=ng[:, l, i, :], op0=ALU.add, op1=ALU.mult),
                         reads=[t_mod, t_ng], writes=[t_coef])
                    R.op("dve", lambda e, i=i, r=r: e.tensor_copy(out=coefB[:, l, i, r, :], in_=mod[:, l, 3 * i, :, r]),
                         reads=[t_mod], writes=[t_coef])
                    gs = 1.0 if i == 1 else 0.5
                    R.op("dve", lambda e, i=i, r=r, gs=gs: e.tensor_scalar(
                            out=coefG[:, l, i, r, :], in0=mod[:, l, 3 * i + 2, :, r], scalar1=gs, scalar2=None,
                            op0=ALU.mult),
                         reads=[t_mod], writes=[t_coef])

        with ExitStack() as st:
            alloc_common(st, with_x=False)
            R.op("pool", lambda e: e.memset(ones[:], 1.0), writes=[t_ones])
            R.op("pool", lambda e: e.memset(epsc[:], EPS), writes=[t_ones])
            R.dma("sp", "c0", lambda e: [e.dma_start(out=ccs.rearrange("p k r -> p (k r)"), in_=cc_d),
                                         e.dma_start(out=bada.rearrange("p a b k r -> p (a b k r)"), in_=bada_d),
                                         e.dma_start(out=ng.rearrange("p a b k -> p (a b k)"), in_=ng_d),
                                         e.dma_start(out=qkg[:], in_=qkg_d),
                                         e.dma_start(out=rm[:], in_=rm_d)], 5,
                  writes=[t_cc, t_bada, t_ng, t_qkg, t_rm])
            R.op("act", lambda e: e.activation(out=sT[:], in_=ccs[:], func=AF.Silu), reads=[t_cc], writes=[t_sT])
            ringA.start([c.a_ada[0] + s for s in range(c.n_ada)])
            for s in range(c.n_ada):
                ada_slab(ringA, 0, s, 1 + (s % 2))
            ada_coefs(0)
            R.barrier()
            R.flush()

        t_a = [Tok(f"a{f}") for f in range(FC)]
        t_sg = [Tok("sg0"), Tok("sg1")]
        nsubB = _split(FC * 128)

        def ffn_phase(st, nB=2):
            aT = sb(st, "aT", [P, FC, T], BF16)
            wB_t = sb(st, "wB", [P, nB, FC, 128], BF16)
            sg = sb(st, "sg", [P, 2, T], F32)
            xT_, hT_ = xT, hT

            def loadB(buf, idx):
                def fn(e):
                    src = WB[idx * P:(idx + 1) * P, :].rearrange("p (a b) -> p a b", a=nsubB)
                    dst = buf.rearrange("p k n -> p (k n)").rearrange("p (a b) -> p a b", a=nsubB)
                    return [e.dma_start(out=dst, in_=src)]
                return fn, 1

            ringB = Ring(R, "wB", [wB_t[:, i] for i in range(nB)], loadB, "pool", cache=cacheB)

            def ffn(l, i, r, Tg):
                sub_i = 0 if i == 0 else 2
                norm_mod(l, sub_i, r, Tg)
                for s in range(c.n_fin):
                    buf, tk = ringA.get()
                    for j in range(2):
                        f = 2 * s + j
                        pg, pu = 3 + (f % 2), 5 + (f % 2)
                        if f == 0:
                            for k in range(DC):
                                R.op("pe", lambda e, k=k, buf=buf, pg=pg: e.matmul(
                                        pss[pg][:, :Tg], lhsT=buf[:, k, 0:128], rhs=hT_[:, k, :Tg],
                                        start=(k == 0), stop=(k == DC - 1)),
                                     reads=[tk, t_h[k]], writes=[t_ps[pg]])
                        else:
                            mm_group(pss[pg][:, :Tg], [(buf[:, k, j * 128:(j + 1) * 128], hT_[:, k, :Tg]) for k in range(DC)],
                                     reads=[tk] + t_h, tok_out=t_ps[pg])
                        mm_group(pss[pu][:, :Tg], [(buf[:, k, 256 + j * 128:256 + (j + 1) * 128], hT_[:, k, :Tg]) for k in range(DC)],
                                 reads=[tk] + t_h, tok_out=t_ps[pu])
                        b = f % 2
                        R.op("act", lambda e, pg=pg, b=b: e.activation(out=sg[:, b, :Tg], in_=pss[pg][:, :Tg], func=AF.Silu),
                             reads=[t_ps[pg]], writes=[t_sg[b]])
                        R.op("dve", lambda e, pu=pu, b=b, f=f: e.tensor_tensor(out=aT[:, f, :Tg], in0=pss[pu][:, :Tg],
                                                                              in1=sg[:, b, :Tg], op=ALU.mult),
                             reads=[t_ps[pu], t_sg[b]], writes=[t_a[f]])
                    ringA.release()
                cons = resid_consume(l, sub_i, r, Tg)
                for oc in range(DC):
                    buf, tk = ringB.get()
                    pi = 1 + (oc % 2)
                    mm_group(pss[pi][:, :Tg], [(buf[:, f, :], aT[:, f, :Tg]) for f in range(FC)],
                             reads=[tk] + t_a, tok_out=t_ps[pi])
                    cons(oc, pss[pi][:, :Tg], t_ps[pi])
                    ringB.release()
            return ffn, ringB, aT, sg

        def fin_seq(l, i):
            return [c.a_fin[(l, i)] + s for s in range(c.n_fin)]

        def fout_seq(l, i):
            return [c.b_fout[(l, i)] + s for s in range(DC)]

        with ExitStack() as st:
            alloc_common(st)
            ffn, ringB, aT, sg = ffn_phase(st, nB=3)
            gl = groups()
            ringA.start([s for _ in gl for s in fin_seq(0, 0)])
            ringB.start([s for _ in gl for s in fout_seq(0, 0)])
            hT_ = hT
            for (g0, Tg, r) in gl:
                load_x(xT0, g0, Tg)
                ffn(0, 0, r, Tg)
                store_x(xs, g0, Tg)
                norm_mod(0, 1, r, Tg)
                R.dma("sp", "h", lambda e, g0=g0, Tg=Tg: [e.dma_start(out=fm(h1)[:, :, g0:g0 + Tg], in_=hT_[:, :, :Tg])], 1,
                      reads=t_h)
            R.barrier()
            R.flush()
        if upto <= 1:
            debug_tail(xs)
            return nc

        with ExitStack() as st:
            NHf = S // 2
            NHp = KH_S * P
            NCHm = max(KH_S, NCH_L)
            hB = sb(st, "hB", [P, 4, S], BF16)
            he = sb(st, "he", [P, 4, NHp], BF16)
            ho = sb(st, "ho", [P, 4, NHp], BF16)
            PQ = sb(st, "PQ", [P, NCHm, 2, 2, 256], BF16)
            wF_t = sb(st, "wF", [P, 3, NCHm, 512], BF16)
            wA2 = sb(st, "wA2", [P, 2, DC, 512], BF16)
            ringA2 = Ring(R, "wA", [wA2[:, i] for i in range(2)], loadA, "pool")
            ringA2.start([c.a_ada[1] + s for s in range(c.n_ada)])
            ada_left = list(range(c.n_ada))
            ztS = sb(st, "ztS", [P, 2, 4, 512], BF16)
            csc = sb(st, "csc", [P, 2, 512], BF16)
            t_hB, t_csc, t_he, t_ho = Tok("hB"), Tok("csc"), Tok("he"), Tok("ho")
            t_PQ = [Tok(f"PQ{n}") for n in range(NCHm)]
            t_zs = [Tok("zs0"), Tok("zs1")]
            R.dma("sp", "c0", lambda e: [e.dma_start(out=csc.rearrange("p a b -> p (a b)"), in_=csc_d)], 1, writes=[t_csc])
            R.op("pool", lambda e: e.memset(he[:, :, NHf + 1:NHp], 0.0), writes=[t_he])
            R.op("pool", lambda e: e.memset(ho[:, :, NHf:NHp], 0.0), writes=[t_ho])
            R.op("pool", lambda e: e.memset(ho[:, :, 0:1], 0.0), writes=[t_ho])

            def loadF(buf, src):
                tab, row, KH, BW = src

                def fn(e):
                    return [e.dma_start(out=buf[:, :KH, :BW], in_=tab[row * P:(row + 1) * P, :].rearrange("p (k n) -> p k n", k=KH))]
                return fn, 1

            ringF = Ring(R, "wF", [wF_t[:, i] for i in range(3)], loadF, "sp")
            NCB = D // 512
            zi = [0]

            def fourier(n, tok0, tab, KH, NHALF, NB, BW, fold, seq_only=False):
                NCHn = KH * NHALF
                if seq_only:
                    seq = []
                    for cb in range(NCB):
                        for nb in range(NB):
                            for cs in range(2):
                                for hf in range(NHALF):
                                    seq.append((tab, (nb * 2 + cs) * NHALF + hf, KH, BW))
                    return seq
                PQf = PQ.rearrange("p n a g c -> p n a (g c)")

                def prep(cb):
                    R.dma("sp", "hB", lambda e: [e.dma_start(out=hB[:, :, :n], in_=fm(h1)[:, cb * 4:(cb + 1) * 4, tok0:tok0 + n])], 1,
                          writes=[t_hB])
                    if fold:
                        R.op("dve", lambda e: e.tensor_tensor(out=he[:, :, 1:NHf], in0=hB[:, :, 1:NHf], in1=hB[:, :, n - 1:NHf:-1], op=ALU.add),
                             reads=[t_hB], writes=[t_he])
                        R.op("dve", lambda e: e.tensor_tensor(out=ho[:, :, 1:NHf], in0=hB[:, :, 1:NHf], in1=hB[:, :, n - 1:NHf:-1], op=ALU.subtract),
                             reads=[t_hB], writes=[t_ho])
                        R.op("act", lambda e: e.activation(out=he[:, :, 0:1], in_=hB[:, :, 0:1], func=AF.Copy), reads=[t_hB], writes=[t_he])
                        R.op("act", lambda e: e.activation(out=he[:, :, NHf:NHf + 1], in_=hB[:, :, NHf:NHf + 1], func=AF.Copy),
                             reads=[t_hB], writes=[t_he])

                def chan(cb):
                    for ntile in range(NCHn):
                        for gg in range(2):
                            pi = 1 + ((ntile * 2 + gg) % 2)
                            tsl = slice(ntile * P, (ntile + 1) * P)
                            if fold:
                                def fn(e, pi=pi, gg=gg, tsl=tsl):
                                    ins = None
                                    for half, srcb in ((0, he), (1, ho)):
                                        for j in range(2):
                                            ins = e.matmul(pss[pi][:, half * 256:(half + 1) * 256], lhsT=srcb[:, 2 * gg + j, tsl],
                                                           rhs=csc[:, j, half * 256:(half + 1) * 256], start=(j == 0), stop=(j == 1))
                                    return ins
                                R.op("pe", fn, reads=[t_he, t_ho, t_csc], writes=[t_ps[pi]])
                            else:
                                mm_group(pss[pi][:, :], [(hB[:, 2 * gg + j, tsl], csc[:, j, :]) for j in range(2)],
                                         reads=[t_hB, t_csc], tok_out=t_ps[pi])
                            if gg == 0:
                                R.op("act", lambda e, pi=pi, ntile=ntile, gg=gg: e.activation(
                                        out=PQ[:, ntile, :, gg, :], in_=pss[pi][:, :].rearrange("p (a b) -> p a b", a=2), func=AF.Copy),
                                     reads=[t_ps[pi]], writes=[t_PQ[ntile]])
                            else:
                                R.op("dve", lambda e, pi=pi, ntile=ntile, gg=gg: e.tensor_copy(
                                        out=PQ[:, ntile, :, gg, :], in_=pss[pi][:, :].rearrange("p (a b) -> p a b", a=2)),
                                     reads=[t_ps[pi]], writes=[t_PQ[ntile]])

                def pos(cb):
                    for nb in range(NB):
                        first = True
                        for cs in range(2):
                            for hf in range(NHALF):
                                buf, tk = ringF.get()
                                last = (cs == 1 and hf == NHALF - 1)
                                for ch in range(4):
                                    def fn(e, ch=ch, buf=buf, cs=cs, hf=hf, first=first, last=last):
                                        ins = None
                                        for k in range(KH):
                                            ins = e.matmul(pss[3 + ch][:, :BW], lhsT=PQf[:, hf * KH + k, cs, ch * P:(ch + 1) * P],
                                                           rhs=buf[:, k, :BW], start=(first and k == 0), stop=(last and k == KH - 1))
                                        return ins
                                    R.op("pe", fn, reads=[tk] + t_PQ[:NCHn], writes=[t_ps[3 + ch]])
                                first = False
                                ringF.release()
                        zb = zi[0] % 2
                        zi[0] += 1
                        for ch in range(4):
                            if ch % 2 == 0:
                                R.op("act", lambda e, ch=ch, zb=zb: e.activation(out=ztS[:, zb, ch, :BW], in_=pss[3 + ch][:, :BW], func=AF.Copy),
                                     reads=[t_ps[3 + ch]], writes=[t_zs[zb]])
                            else:
                                R.op("dve", lambda e, ch=ch, zb=zb: e.tensor_copy(out=ztS[:, zb, ch, :BW], in_=pss[3 + ch][:, :BW]),
                                     reads=[t_ps[3 + ch]], writes=[t_zs[zb]])
                        R.dma("sp", f"zs{zb}", lambda e, zb=zb, cb=cb, nb=nb: [e.dma_start(
                                out=fm(zt)[:, cb * 4:(cb + 1) * 4, tok0 + nb * BW:tok0 + (nb + 1) * BW], in_=ztS[:, zb, :, :BW])], 1,
                              reads=[t_zs[zb]])
                        for _ in range(2):
                            if ada_left:
                                s_ = ada_left.pop(0)
                                ada_slab(ringA2, 1, s_, 1 + (s_ % 2))

                prep(0)
                for cb in range(NCB):
                    chan(cb)
                    if cb + 1 < NCB:
                        prep(cb + 1)
                    pos(cb)

            argsS = (S, 0, dftS_d, KH_S, 1, NB_S, 512, True)
            argsL = (L, S, dftL_d, NCH_L, 1, 1, L, False)
            ringF.start(fourier(*argsS, seq_only=True) + fourier(*argsL, seq_only=True))
            fourier(*argsS)
            fourier(*argsL)
            while ada_left:
                s_ = ada_left.pop(0)
                ada_slab(ringA2, 1, s_, 1 + (s_ % 2))
            ada_coefs(1)
            R.barrier()
            R.flush()
        if upto <= 2:
            debug_tail(zt, bf=True)
            return nc

        with ExitStack() as st:
            alloc_common(st)
            ffn, ringB, aT, sg = ffn_phase(st, nB=3)
            xT_, hT_ = xT, hT
            rope = sb(st, "rope", [P, 2, T], F32)
            kv_alias = FC >= NH + DC + NKV + 4
            if kv_alias:
                o1 = NH + DC
                kS, t_kSl = aT[:, o1:o1 + NKV, :], t_a[o1:o1 + NKV]
                vS, t_vSl = aT[:, o1 + NKV:o1 + NKV + 4, :KVW], t_a[o1 + NKV:o1 + NKV + 4]
            else:
                kS = sb(st, "kS", [P, NKV, T], BF16)
                vS = sb(st, "vS", [P, 4, KVW], BF16)
                t_kSl, t_vSl = [Tok("kS")], [Tok("vS")]
            qn, t_qn = nt_, t_nt
            t1, t_t1 = sg, t_sg
            qS = aT[:, 0:NH, :].rearrange("p f t -> p (f t)").rearrange("p (k a g t) -> p k a g t", k=NKV, a=4, g=4)
            t_qS = t_a[0:NH]
            t_rope = Tok("rope")
            gl = groups()
            seqA, seqB = [], []
            for (g0, Tg, r) in gl:
                seqA += [c.a_wf + s for s in range(c.n_sq)]
                seqA += fin_seq(0, 1) + fin_seq(1, 0)
                if r == 0:
                    seqA += [c.a_qkv + s for s in range(c.n_sq + 2)]
                else:
                    seqA += [c.a_qkv + c.n_sq, c.a_qkv + c.n_sq + 1]
                seqB += fout_seq(0, 1) + fout_seq(1, 0)
            ringA.start(seqA)
            ringB.start(seqB)
            qt_v = qt.rearrange("p (k q g t) -> p k q g t", k=NKV, g=4, t=P)
            kt_v = kt.rearrange("p (k n) -> p k n", k=NKV)
            QB, NB_, RB = (1, 2), (0, 5), (7, 6)

            def qk_pipeline(jobs, Tg):
                cur = {}
                rstd_ = rstd

                def Pj(h):
                    jb = jobs[h]
                    if jb["first"]:
                        cur["buf"], cur["tk"] = ringA.get()
                    buf, tk, j = cur["buf"], cur["tk"], jb["j"]
                    pi = QB[h % 2]
                    mm_group(pss[pi][:, :Tg], [(buf[:, k, j * 128:(j + 1) * 128], hT_[:, k, :Tg]) for k in range(DC)],
                             reads=[tk] + t_h, tok_out=t_ps[pi])
                    if jb["last"]:
                        ringA.release()

                def Nj(h):
                    jb = jobs[h]
                    b = h % 2
                    pi = QB[b]
                    ps = pss[pi][:, :Tg]
                    rms_rstd([ps], [t_ps[pi]], Tg, 128, NB_[b], rb=b)
                    gi = jb["gi"]
                    if jb["rope"]:
                        R.op("dve", lambda e: e.scalar_tensor_tensor(out=qn[:, b, :Tg], in0=ps, scalar=qkg[:, gi:gi + 1],
                                                                     in1=rstd_[:, b, :Tg], op0=ALU.mult, op1=ALU.mult),
                             reads=[t_ps[pi], t_rstd[b], t_qkg], writes=[t_qn[b]])
                    else:
                        oc = jb["oc"]
                        R.op("dve", lambda e: e.scalar_tensor_tensor(out=kS[:, oc, :Tg], in0=ps, scalar=qkg[:, gi:gi + 1],
                                                                     in1=rstd_[:, b, :Tg], op0=ALU.mult, op1=ALU.mult),
                             reads=[t_ps[pi], t_rstd[b], t_qkg], writes=t_kSl)

                def Fj(h):
                    jb = jobs[h]
                    if not jb["rope"]:
                        return
                    b = h % 2
                    pr = RB[b]
                    R.op("pe", lambda e: e.matmul(pss[pr][:, :Tg], lhsT=rm[:], rhs=qn[:, b, :Tg], start=True, stop=True),
                         reads=[t_rm, t_qn[b]], writes=[t_ps[pr]])
                    R.op("dve", lambda e: e.tensor_tensor(out=t1[:, b, :Tg], in0=qn[:, b, :Tg], in1=rope[:, 0, :Tg], op=ALU.mult),
                         reads=[t_qn[b], t_rope], writes=[t_t1[b]])
                    R.op("dve", lambda e: e.tensor_tensor(out=qn[:, b, :Tg], in0=pss[pr][:, :Tg], in1=rope[:, 1, :Tg], op=ALU.mult),
                         reads=[t_ps[pr], t_rope, t_qn[b]], writes=[t_qn[b]])
                    oc = jb["oc"]
                    if jb["kind"] == "q":
                        kvh, g = oc // 4, oc % 4
                        R.op("dve", lambda e: e.tensor_tensor(out=qS[:, kvh, :, g, :],
                                                              in0=t1[:, b, :Tg].rearrange("p (a t) -> p a t", t=P),
                                                              in1=qn[:, b, :Tg].rearrange("p (a t) -> p a t", t=P), op=ALU.add),
                             reads=[t_t1[b], t_qn[b]], writes=t_qS)
                    else:
                        R.op("dve", lambda e: e.tensor_tensor(out=kS[:, oc, :Tg], in0=t1[:, b, :Tg], in1=qn[:, b, :Tg], op=ALU.add),
                             reads=[t_t1[b], t_qn[b]], writes=t_kSl)

                n = len(jobs)
                Pj(0)
                Nj(0)
                for h in range(1, n):
                    Pj(h)
                    Fj(h - 1)
                    Nj(h)
                Fj(n - 1)

            zpre = FC >= NH + DC
            zT = aT[:, NH:NH + DC, :] if zpre else hT_
            t_z = t_a[NH:NH + DC] if zpre else t_h

            def load_z(g0, Tg):
                R.dma("sp", "h", lambda e: [e.dma_start(out=zT[:, :, :Tg], in_=fm(zt)[:, :, g0:g0 + Tg])], 1, writes=t_z)

            if zpre:
                load_z(*gl[0][:2])
            for gi_, (g0, Tg, r) in enumerate(gl):
                if gi_ == 0:
                    load_x(xs, g0, Tg)
                if not zpre:
                    load_z(g0, Tg)
                if r == 0:
                    R.dma("sp", "rope", lambda e, g0=g0: [e.dma_start(
                            out=rope[:], in_=rope_d.rearrange("p (a n) -> p a n", a=2)[:, :, g0:g0 + T])], 1, writes=[t_rope])
                proj(c.n_sq, zT, t_z, Tg, resid_consume(0, 1, r, Tg))
                ffn(0, 1, r, Tg)
                ffn(1, 0, r, Tg)
                norm_mod(1, 1, r, Tg)
                store_x(xs, g0, Tg)
                if gi_ + 1 < len(gl):
                    load_x(xs, *gl[gi_ + 1][:2])
                    if zpre:
                        load_z(*gl[gi_ + 1][:2])
                jobs = []
                if r == 0:
                    for s in range(c.n_sq):
                        for j in range(4):
                            jobs.append(dict(first=(j == 0), last=(j == 3), j=j, gi=0, rope=True, oc=s * 4 + j, kind="q"))
                for j in range(NKV):
                    jobs.append(dict(first=(j == 0), last=(j == NKV - 1), j=j, gi=1, rope=(r == 0), oc=j, kind="k"))
                qk_pipeline(jobs, Tg)
                if r == 0:
                    R.dma("sp", "qS", lambda e, g0=g0: [e.dma_start(out=qt_v[:, :, g0 // P:g0 // P + 4, :, :], in_=qS)], 1,
                          reads=t_qS)
                R.dma("sp", "kS", lambda e, g0=g0, Tg=Tg: [e.dma_start(out=kt_v[:, :, g0:g0 + Tg], in_=kS[:, :, :Tg])], 1,
                      reads=t_kSl)
                buf, tk = ringA.get()
                for tt in range(Tg // P):
                    pi = 1 + (tt % 2)
                    mm_group(pss[pi][:, :KVW], [(hT_[:, k, tt * P:(tt + 1) * P], buf[:, k, :KVW]) for k in range(DC)],
                             reads=[tk] + t_h, tok_out=t_ps[pi])
                    if tt % 2 == 0:
                        R.op("act", lambda e, pi=pi, tt=tt: e.activation(out=vS[:, tt, :], in_=pss[pi][:, :KVW], func=AF.Copy),
                             reads=[t_ps[pi]], writes=t_vSl)
                    else:
                        R.op("dve", lambda e, pi=pi, tt=tt: e.tensor_copy(out=vS[:, tt, :], in_=pss[pi][:, :KVW]),
                             reads=[t_ps[pi]], writes=t_vSl)
                ringA.release()
                R.dma("sp", "vS", lambda e, g0=g0, Tg=Tg: [e.dma_start(
                        out=vv[g0:g0 + Tg, :].rearrange("(a p) f -> p a f", p=P), in_=vS[:, :Tg // P, :])], 1,
                      reads=t_vSl)
            R.barrier()
            R.flush()
        if upto <= 3:
            debug_tail(xs)
            return nc

        with ExitStack() as st:
            alloc_common(st, nA=2)
            xT_, hT_ = xT, hT
            NK = NT
            NKC = NK // P
            KT = sb(st, "KT", [P, NKV, NK], BF16)
            V = sb(st, "V", [P, NKC, KVW], BF16)
            qG = sb(st, "qG", [P, NKV, 4, 4 * P], BF16)
            NPT = 8
            pT = sb(st, "pT", [P, NPT, 512], BF16)
            rden = sb(st, "rden", [P, 512], F32)
            accs = sb(st, "accs", [P, 512], F32)
            accb = sb(st, "accb", [P, 512], BF16)
            t_KT, t_V, t_qG, t_rden, t_accs, t_accb = Tok("KT"), Tok("V"), Tok("qG"), Tok("rden"), Tok("accs"), Tok("accb")
            t_pT = [Tok(f"pT{i}") for i in range(NPT)]
            R.dma("sp", "kv", lambda e: [e.dma_start(out=KT[:], in_=kt.rearrange("p (k n) -> p k n", k=NKV)),
                                         e.dma_start(out=V[:], in_=vv.rearrange("(a p) f -> p a f", p=P))], 2,
                  writes=[t_KT, t_V])
            gl = groups(False)
            ringA.start([c.a_wo + s for _ in gl for s in range(c.n_sq)])
            qt_g = qt.rearrange("p (k q f) -> p k q f", k=NKV, f=4 * P)
            scale = 1.0 / math.sqrt(128.0)
            SB = (1, 2, 7)
            blk = 0
            pending = [None]
            bg_jobs = [("A", i_) for i_ in fin_seq(1, 1)] + [("B", i_) for i_ in fout_seq(1, 1)]
            n_blocks = len(gl) * NKV * 4
            bg_per_blk = -(-len(bg_jobs) // max(1, n_blocks - 1))

            def bg_convert(kind, idx):
                if kind == "A":
                    s_ = WA[idx * P:(idx + 1) * P, :].rearrange("p (a b) -> p a b", a=nsubA)
                    d_ = WAc[(idx - a0) * P:(idx - a0 + 1) * P, :].rearrange("p (a b) -> p a b", a=nsubA)
                    ch = cacheA
                else:
                    s_ = WB[idx * P:(idx + 1) * P, :].rearrange("p (a b) -> p a b", a=nsubB)
                    d_ = WBc[idx * P:(idx + 1) * P, :].rearrange("p (a b) -> p a b", a=nsubB)
                    ch = cacheB
                if idx in ch["seen"]:
                    return
                R.dma("pool", "bgc", lambda e: [e.dma_start(out=d_, in_=s_)], 1, writes=[ch["tok"]])
                ch["seen"].add(idx)

            def tail2(args):
                po, kvh, tt, st0 = args
                R.op("pe", lambda e: e.matmul(pss[0][:, :], lhsT=ones[:], rhs=accb[:], start=st0, stop=True),
                     reads=[t_accb, t_ones], writes=[t_ps[0]])
                R.op("act", lambda e: e.activation(out=rden[:], in_=pss[0][:, :], func=AF.Ln),
                     reads=[t_ps[0]], writes=[t_rden])
                R.op("act", lambda e: e.activation(out=rden[:], in_=rden[:], func=AF.Exp, scale=-1.0),
                     reads=[t_rden], writes=[t_rden])
                R.op("dve", lambda e: e.tensor_tensor(
                        out=hT_[:, 4 * kvh:4 * kvh + 4, tt * P:(tt + 1) * P],
                        in0=pss[po][:, :].rearrange("p (g t) -> p g t", g=4),
                        in1=rden.rearrange("p (g t) -> p g t", g=4), op=ALU.mult),
                     reads=[t_ps[po], t_rden], writes=t_h[4 * kvh:4 * kvh + 4])

            for (g0, Tg, r) in gl:
                R.dma("sp", "qG", lambda e, g0=g0: [e.dma_start(out=qG[:], in_=qt_g[:, :, g0 // P:g0 // P + 4, :])], 1,
                      writes=[t_qG])
                load_x(xs, g0, Tg)
                for kvh in range(NKV):
                    for tt in range(4):
                        po = 3 + (blk % 2)
                        blk += 1
                        qb = qG[:, kvh, tt, :]

                        def s_mm(kc, kvh=kvh, qb=qb):
                            pi = SB[kc % 3]
                            R.op("pe", lambda e: e.matmul(pss[pi][:, :], lhsT=KT[:, kvh, kc * P:(kc + 1) * P], rhs=qb,
                                                          start=True, stop=True),
                                 reads=[t_KT, t_qG], writes=[t_ps[pi]])
                        s_mm(0)
                        if NKC > 1:
                            s_mm(1)
                        pe_den = [kc for kc in range(NKC) if kc % 4 == 3]
                        dve_cnt = 0
                        for kc in range(NKC):
                            if kc + 2 < NKC:
                                s_mm(kc + 2)
                            pi = SB[kc % 3]
                            pb = kc % NPT
                            R.op("act", lambda e, pi=pi, pb=pb: e.activation(out=pT[:, pb, :], in_=pss[pi][:, :], func=AF.Exp, scale=scale),
                                 reads=[t_ps[pi]], writes=[t_pT[pb]])
                            on_pe = kc in pe_den

                            def pv(e, kc=kc, kvh=kvh, pb=pb, po=po, on_pe=on_pe):
                                ins = e.matmul(pss[po][:, :], lhsT=V[:, kc, kvh * P:(kvh + 1) * P], rhs=pT[:, pb, :],
                                               start=(kc == 0), stop=(kc == NKC - 1))
                                if on_pe:
                                    ins = e.matmul(pss[0][:, :], lhsT=ones[:], rhs=pT[:, pb, :],
                                                   start=(kc == pe_den[0]), stop=False)
                                return ins
                            R.op("pe", pv, reads=[t_V, t_pT[pb], t_ones], writes=[t_ps[po]] + ([t_ps[0]] if on_pe else []))
                            if not on_pe:
                                a = 5 + (dve_cnt % 2)
                                if dve_cnt < 2:
                                    R.op("dve", lambda e, a=a, pb=pb: e.tensor_copy(out=pss[a][:, :], in_=pT[:, pb, :]),
                                         reads=[t_pT[pb]], writes=[t_ps[a]])
                                else:
                                    R.op("dve", lambda e, a=a, pb=pb: e.tensor_tensor(out=pss[a][:, :], in0=pss[a][:, :], in1=pT[:, pb, :], op=ALU.add),
                                         reads=[t_pT[pb], t_ps[a]], writes=[t_ps[a]])
                                dve_cnt += 1
                            if kc == 1 and pending[0] is not None:
                                tail2(pending[0])
                                pending[0] = None
                        if dve_cnt >= 2:
                            R.op("dve", lambda e: e.tensor_copy(out=accs[:], in_=pss[6][:, :]), reads=[t_ps[6]], writes=[t_accs])
                            R.op("dve", lambda e: e.tensor_tensor(out=accb[:], in0=pss[5][:, :], in1=accs[:], op=ALU.add),
                                 reads=[t_ps[5], t_accs], writes=[t_accb])
                        else:
                            R.op("dve", lambda e: e.tensor_copy(out=accb[:], in_=pss[5][:, :]), reads=[t_ps[5]], writes=[t_accb])
                        assert pending[0] is None
                        pending[0] = (po, kvh, tt, len(pe_den) == 0)
                        for _ in range(bg_per_blk):
                            if bg_jobs:
                                bg_convert(*bg_jobs.pop(0))
                tail2(pending[0])
                pending[0] = None
                proj(c.n_sq, hT_, t_h, Tg, resid_consume(1, 1, 0, Tg), ps_ids=(0, 7))
                store_x(xs, g0, Tg)
            R.barrier()
            R.flush()
        if upto <= 4:
            debug_tail(xs)
            return nc

        with ExitStack() as st:
            alloc_common(st)
            ffn, ringB, aT, sg = ffn_phase(st, nB=3)
            gl = groups(False)
            ringA.start([s for _ in gl for s in fin_seq(1, 1)])
            ringB.start([s for _ in gl for s in fout_seq(1, 1)])
            for (g0, Tg, r) in gl:
                load_x(xs, g0, Tg)
                ffn(1, 1, 0, Tg)
                store_x(outT, g0, Tg)
            R.barrier()
            R.flush()
    return nc


def _fm_vec(v):
    v = np.asarray(v, np.float32)
    sh = v.shape[:-1]
    dc = v.shape[-1] // P
    return np.moveaxis(v.reshape(*sh, dc, P), -1, 0)


def _slabA(w):
    din, dout = w.shape
    dc = din // P
    return np.ascontiguousarray(w.reshape(dc, P, dout // 512, 512).transpose(2, 1, 0, 3))


def host_consts(c):
    bf = ml_dtypes.bfloat16
    out = {}
    a = np.arange(256)
    ang = 2.0 * np.pi * ((a[:, None] * a[None, :]) % 256) / 256.0
    cs = np.concatenate([np.cos(ang), -np.sin(ang)], axis=1) / 16.0
    out["csc"] = np.ascontiguousarray(cs.reshape(2, P, 512).transpose(1, 0, 2).reshape(P, 1024)).astype(bf)

    def dft_tab(n, KH, BW, fold=False):
        a = np.arange(n, dtype=np.int64)
        ang = 2.0 * np.pi * ((a[:, None] * a[None, :]) % n).astype(np.float64) / n
        tabs = np.stack([np.cos(ang), np.sin(ang)], 0) / math.sqrt(n)
        if fold:
            t2 = np.zeros((2, KH * P, n))
            t2[:, :n // 2 + 1] = tabs[:, :n // 2 + 1]
            tabs = t2
        NHALF = tabs.shape[1] // P // KH
        NB = n // BW
        t = tabs.reshape(2, NHALF, KH, P, NB, BW).transpose(4, 0, 1, 3, 2, 5)
        return np.ascontiguousarray(t).reshape(NB * 2 * NHALF * P, KH * BW).astype(bf)

    KH_S = (c.S // 2) // P + 1
    out["dftS"] = dft_tab(c.S, KH_S, 512, fold=True)
    out["dftL"] = dft_tab(c.L, c.L // P, c.L)
    rf = 32
    t = np.arange(c.S)
    row = (t // c.GRID_W).astype(np.float32)
    col = (t % c.GRID_W).astype(np.float32)
    inv = (np.float32(c.THETA) ** (-np.arange(rf, dtype=np.float32) / np.float32(rf))).astype(np.float32)
    a_r = row[:, None] * inv
    a_c = col[:, None] * inv
    ang = np.concatenate([a_r, a_r, a_c, a_c], axis=-1)
    out["rope"] = np.ascontiguousarray(np.stack([np.cos(ang).T, np.sin(ang).T], 1).reshape(P, 2 * c.S)).astype(np.float32)
    rm = np.zeros((P, P), np.float32)
    for j in range(P):
        if (j // 32) % 2 == 0:
            rm[j + 32, j] = -1.0
        else:
            rm[j - 32, j] = 1.0
    out["rm"] = rm
    return out


def host_weights(c, w_ada, b_ada, norm_g, w_ffn_in, w_ffn_out, w_fourier_out, w_qkv, q_norm_g, k_norm_g, w_attn_out, c_ctx):
    D, FF = c.D, c.FF
    slabs = []
    for l in range(2):
        slabs.append(_slabA(np.asarray(w_ada[l], np.float32)))
    for l in range(2):
        for i in range(2):
            w = np.asarray(w_ffn_in[l, i], np.float32)
            g = w[:, :FF].reshape(c.DC, P, c.n_fin, 256)
            u = w[:, FF:].reshape(c.DC, P, c.n_fin, 256)
            s = np.concatenate([g, u], axis=-1).transpose(2, 1, 0, 3)
            slabs.append(np.ascontiguousarray(s))
    slabs.append(_slabA(np.asarray(w_fourier_out[0], np.float32)))
    wq = np.asarray(w_qkv[0], np.float32)
    kvw = c.NKV * P
    slabs.append(_slabA(wq[:, :D]))
    for part in (wq[:, D:D + kvw], wq[:, D + kvw:D + 2 * kvw]):
        pad = np.zeros((D, 512), np.float32)
        pad[:, :kvw] = part
        slabs.append(_slabA(pad))
    slabs.append(_slabA(np.asarray(w_attn_out[0], np.float32)))
    WA = np.concatenate(slabs, axis=0)
    assert WA.shape[0] == c.n_a, (WA.shape, c.n_a)
    WA = WA.reshape(c.n_a * P, c.DC * 512)
    wb = []
    for l in range(2):
        for i in range(2):
            w = np.asarray(w_ffn_out[l, i], np.float32)
            wb.append(np.ascontiguousarray(w.reshape(c.FC, P, c.DC, P).transpose(2, 1, 0, 3)))
    WB = np.concatenate(wb, axis=0).reshape(c.n_b * P, c.FC * P)
    bd = _fm_vec(np.asarray(b_ada, np.float32).reshape(2, 9, D))
    bada = np.ascontiguousarray(np.repeat(bd[..., None], 2, axis=-1)).reshape(P, -1)
    ngv = np.ascontiguousarray(_fm_vec(np.asarray(norm_g, np.float32))).reshape(P, -1)
    qkg = np.ascontiguousarray(np.stack([np.asarray(q_norm_g, np.float32)[0], np.asarray(k_norm_g, np.float32)[0]], axis=1))
    return dict(WA=WA, WB=WB, bada=bada, ng=ngv, qkg=qkg)


_CACHE = {}


def run(cfg, x, c, ctx, c_ctx, w_ada, b_ada, norm_g, w_ffn_in, w_ffn_out, w_fourier_out, w_qkv,
        q_norm_g, k_norm_g, w_attn_out, upto=99, cores=None):
    B = x.shape[0]
    cores = list(range(B)) if cores is None else cores
    shared = host_consts(cfg)
    shared.update(host_weights(cfg, w_ada, b_ada, norm_g, w_ffn_in, w_ffn_out, w_fourier_out, w_qkv,
                               q_norm_g, k_norm_g, w_attn_out, c_ctx))
    x = np.asarray(x, np.float32)
    ctx = np.asarray(ctx, np.float32)
    cv = np.asarray(c, np.float32)
    ccx = np.asarray(c_ctx, np.float32)
    in_maps = []
    for b in cores:
        m = dict(shared)
        m["xT0"] = np.ascontiguousarray(np.concatenate([x[b].T, ctx[b].T], axis=1))
        m["cc"] = np.ascontiguousarray(np.stack([_fm_vec(cv[b]), _fm_vec(ccx)], axis=-1)).reshape(P, -1)
        in_maps.append(m)
    key = (cfg.D, cfg.FF, cfg.S, cfg.L, upto)
    if key not in _CACHE:
        _CACHE[key] = build(cfg, upto)
    nc = _CACHE[key]
    res = run_bass_kernel_spmd(nc, in_maps, core_ids=list(range(len(cores))))
    out = np.stack([np.ascontiguousarray(r["outT"].T) for r in res.results], axis=0)
    return out.astype(np.float32)


def kernel(x, c, ctx, c_ctx, w_ada, b_ada, norm_g, w_ffn_in, w_ffn_out, w_fourier_out, w_qkv,
           q_norm_g, k_norm_g, w_attn_out):
    cfg = Cfg()
    return run(cfg, x, c, ctx, c_ctx, w_ada, b_ada, norm_g, w_ffn_in, w_ffn_out, w_fourier_out, w_qkv,
               q_norm_g, k_norm_g, w_attn_out)
```
